# Optimizing a Trainium2 kernel written in Bass

```python
import math
import jax, jax.numpy as jnp
from jax import lax
import numpy as np

D_MODEL = 4096
BATCH = 4
SEQ = 4096
DEPTH = 1

CHUNK = 64

FOX_HEADS = 16
FOX_HEAD_DIM = 128
FOX_WIDTH = FOX_HEADS * FOX_HEAD_DIM
Q_BLOCK = 128

S5_GROUP = 16
S5_GROUPS = 64
S5_WIDTH = S5_GROUP * S5_GROUPS
S5_STATE = 64

D_FF = ((8 * D_MODEL + 3 * 256 - 1) // (3 * 256)) * 256

N_BRANCHES = 2
RMS_EPS = 1e-6
MASK_VALUE = -1e30

COL_Q = FOX_WIDTH
COL_K = COL_Q + FOX_WIDTH
COL_V = COL_K + FOX_WIDTH
COL_F = COL_V + FOX_HEADS
COL_S5 = COL_F + S5_WIDTH
COL_GATE_FOX = COL_S5 + D_MODEL
IN_COLS = COL_GATE_FOX + D_MODEL

kernel_name = "fox_s5_gated_hybrid_block"


def _rmsnorm(x, g):
    x32 = x.astype(jnp.float32)
    y = x32 * lax.rsqrt(jnp.mean(x32 * x32, axis=-1, keepdims=True) + RMS_EPS)
    return (y * g.astype(jnp.float32)).astype(x.dtype)


def _forgetting_attention(q, k, v, f_logit, q_norm, k_norm):
    B, S, _ = q.shape
    n_blk = S // Q_BLOCK
    q = _rmsnorm(q.reshape(B, S, FOX_HEADS, FOX_HEAD_DIM), q_norm) * (1.0 / math.sqrt(FOX_HEAD_DIM))
    k = _rmsnorm(k.reshape(B, S, FOX_HEADS, FOX_HEAD_DIM), k_norm)
    v = v.reshape(B, S, FOX_HEADS, FOX_HEAD_DIM)
    qh = q.transpose(0, 2, 1, 3)
    kh = k.transpose(0, 2, 1, 3)
    vh = v.transpose(0, 2, 1, 3)
    log_f = jax.nn.log_sigmoid(f_logit.astype(jnp.float32))
    F = jnp.cumsum(log_f, axis=1).transpose(0, 2, 1)
    q_blocks = qh.reshape(B, FOX_HEADS, n_blk, Q_BLOCK, FOX_HEAD_DIM).transpose(2, 0, 1, 3, 4)
    f_blocks = F.reshape(B, FOX_HEADS, n_blk, Q_BLOCK).transpose(2, 0, 1, 3)
    k_pos = jnp.arange(S)

    def one_block(args):
        qb, fb, blk = args
        q_pos = blk * Q_BLOCK + jnp.arange(Q_BLOCK)
        logits = jnp.einsum('bhqd,bhkd->bhqk', qb, kh).astype(jnp.float32)
        logits = logits + (fb[..., :, None] - F[:, :, None, :])
        logits = jnp.where(k_pos[None, :] <= q_pos[:, None], logits, MASK_VALUE)
        p = jax.nn.softmax(logits, axis=-1)
        return jnp.einsum('bhqk,bhkd->bhqd', p.astype(vh.dtype), vh)

    out = lax.map(one_block, (q_blocks, f_blocks, jnp.arange(n_blk)))
    return out.transpose(1, 0, 3, 2, 4).reshape(B, S, FOX_WIDTH)


def _cplx_scan_op(e1, e2):
    a1r, a1i, b1r, b1i = e1
    a2r, a2i, b2r, b2i = e2
    ar = a2r * a1r - a2i * a1i
    ai = a2r * a1i + a2i * a1r
    br = a2r * b1r - a2i * b1i + b2r
    bi = a2r * b1i + a2i * b1r + b2i
    return (ar, ai, br, bi)


def _s5(s5_in, lam_re, lam_im, log_step, b_re, b_im, c_re, c_im, d, w_glu, b_glu):
    B, S, _ = s5_in.shape
    u = s5_in.reshape(B, S, S5_GROUPS, S5_GROUP).astype(jnp.float32)
    lr = lam_re.astype(jnp.float32)
    li = lam_im.astype(jnp.float32)
    dt = jnp.exp(log_step.astype(jnp.float32))[:, None]
    mag = jnp.exp(lr * dt)
    lb_re = mag * jnp.cos(li * dt)
    lb_im = mag * jnp.sin(li * dt)
    denom = lr * lr + li * li
    num_re = lb_re - 1.0
    fac_re = (num_re * lr + lb_im * li) / denom
    fac_im = (lb_im * lr - num_re * li) / denom
    br = b_re.astype(jnp.float32)
    bi = b_im.astype(jnp.float32)
    bb_re = fac_re[..., None] * br - fac_im[..., None] * bi
    bb_im = fac_re[..., None] * bi + fac_im[..., None] * br
    bu_re = jnp.einsum('bsgi,gpi->bsgp', u, bb_re)
    bu_im = jnp.einsum('bsgi,gpi->bsgp', u, bb_im)
    a_re = jnp.broadcast_to(lb_re[None, None], (1, S, S5_GROUPS, S5_STATE))
    a_im = jnp.broadcast_to(lb_im[None, None], (1, S, S5_GROUPS, S5_STATE))
    _, _, x_re, x_im = lax.associative_scan(_cplx_scan_op, (a_re, a_im, bu_re, bu_im), axis=1)
    y = (jnp.einsum('bsgp,gip->bsgi', x_re, c_re.astype(jnp.float32))
         - jnp.einsum('bsgp,gip->bsgi', x_im, c_im.astype(jnp.float32))
         + d.astype(jnp.float32) * u)
    y = y.reshape(B, S, S5_WIDTH).astype(s5_in.dtype)
    g = jax.nn.gelu(y)
    return g * jax.nn.sigmoid(g @ w_glu + b_glu)


def _layer(x, g_mix, w_in, b_fgate, b_gates, q_norm, k_norm,
           s5_lambda_re, s5_lambda_im, s5_log_step, s5_b_re, s5_b_im, s5_c_re, s5_c_im, s5_d,
           w_glu, b_glu, w_proj_fox, w_proj_s5, w_out, g_ffn, w_gate_up, w_down):
    u = _rmsnorm(x, g_mix)
    z = u @ w_in
    q, k, v, f_logit, s5_in, gate_logits = jnp.split(
        z, [COL_Q, COL_K, COL_V, COL_F, COL_S5], axis=-1)
    gates = jax.nn.sigmoid(gate_logits.astype(jnp.float32) + b_gates.astype(jnp.float32))
    gate_fox, gate_s5 = jnp.split(gates, [D_MODEL], axis=-1)
    attn = _forgetting_attention(q, k, v, f_logit + b_fgate, q_norm, k_norm)
    ssm = _s5(s5_in, s5_lambda_re, s5_lambda_im, s5_log_step, s5_b_re, s5_b_im,
              s5_c_re, s5_c_im, s5_d, w_glu, b_glu)
    merged = (gate_fox * (attn @ w_proj_fox) + gate_s5 * (ssm @ w_proj_s5)).astype(x.dtype)
    h = x + merged @ w_out
    hn = _rmsnorm(h, g_ffn)
    gate, up = jnp.split(hn @ w_gate_up, [D_FF], axis=-1)
    return h + (jax.nn.silu(gate) * up) @ w_down


def _normal(key, shape, scale):
    return jax.random.normal(key, shape, jnp.float32) * scale


def setup_inputs(seed: int = 0) -> dict:
    key = jax.random.key(seed)
    ks = jax.random.split(key, 24)
    L = DEPTH
    n_idx = jnp.arange(S5_STATE, dtype=jnp.float32)
    lam_re = -0.5 + _normal(ks[8], (L, S5_GROUPS, S5_STATE), 0.01)
    lam_im = math.pi * n_idx[None, None, :] + _normal(ks[9], (L, S5_GROUPS, S5_STATE), 0.01)
    log_step = jax.random.uniform(ks[10], (L, S5_GROUPS), jnp.float32,
                                  math.log(1e-3), math.log(1e-1))
    return {
        "x": _normal(ks[0], (BATCH, SEQ, D_MODEL), 1.0),
        "g_mix": 1.0 + _normal(ks[1], (L, D_MODEL), 0.02),
        "w_in": _normal(ks[2], (L, D_MODEL, IN_COLS), D_MODEL ** -0.5),
        "b_fgate": 3.0 + _normal(ks[3], (L, FOX_HEADS), 0.5),
        "b_gates": _normal(ks[4], (L, N_BRANCHES * D_MODEL), 0.02),
        "q_norm": 1.0 + _normal(ks[5], (L, FOX_HEAD_DIM), 0.02),
        "k_norm": 1.0 + _normal(ks[6], (L, FOX_HEAD_DIM), 0.02),
        "s5_lambda_re": lam_re,
        "s5_lambda_im": lam_im,
        "s5_log_step": log_step,
        "s5_b_re": _normal(ks[11], (L, S5_GROUPS, S5_STATE, S5_GROUP), (2 * S5_GROUP) ** -0.5),
        "s5_b_im": _normal(ks[12], (L, S5_GROUPS, S5_STATE, S5_GROUP), (2 * S5_GROUP) ** -0.5),
        "s5_c_re": _normal(ks[13], (L, S5_GROUPS, S5_GROUP, S5_STATE), S5_STATE ** -0.5),
        "s5_c_im": _normal(ks[14], (L, S5_GROUPS, S5_GROUP, S5_STATE), S5_STATE ** -0.5),
        "s5_d": _normal(ks[15], (L, S5_GROUPS, S5_GROUP), 1.0),
        "w_glu": _normal(ks[16], (L, S5_WIDTH, S5_WIDTH), S5_WIDTH ** -0.5),
        "b_glu": _normal(ks[17], (L, S5_WIDTH), 0.02),
        "w_proj_fox": _normal(ks[18], (L, FOX_WIDTH, D_MODEL), FOX_WIDTH ** -0.5),
        "w_proj_s5": _normal(ks[19], (L, S5_WIDTH, D_MODEL), S5_WIDTH ** -0.5),
        "w_out": _normal(ks[20], (L, D_MODEL, D_MODEL), D_MODEL ** -0.5),
        "g_ffn": 1.0 + _normal(ks[21], (L, D_MODEL), 0.02),
        "w_gate_up": _normal(ks[22], (L, D_MODEL, 2 * D_FF), D_MODEL ** -0.5),
        "w_down": _normal(ks[23], (L, D_FF, D_MODEL), D_FF ** -0.5),
    }


def reference(x, g_mix, w_in, b_fgate, b_gates, q_norm, k_norm,
              s5_lambda_re, s5_lambda_im, s5_log_step, s5_b_re, s5_b_im, s5_c_re, s5_c_im, s5_d,
              w_glu, b_glu, w_proj_fox, w_proj_s5, w_out, g_ffn, w_gate_up, w_down):
    for l in range(DEPTH):
        x = _layer(x, g_mix[l], w_in[l], b_fgate[l], b_gates[l], q_norm[l], k_norm[l],
                   s5_lambda_re[l], s5_lambda_im[l], s5_log_step[l], s5_b_re[l], s5_b_im[l],
                   s5_c_re[l], s5_c_im[l], s5_d[l], w_glu[l], b_glu[l],
                   w_proj_fox[l], w_proj_s5[l], w_out[l], g_ffn[l], w_gate_up[l], w_down[l])
    return x
```

```python
import math
import numpy as np
import ml_dtypes
import concourse.bass as bass
import concourse.mybir as mybir
from concourse.bass_utils import run_bass_kernel_spmd

F32 = mybir.dt.float32
BF16 = mybir.dt.bfloat16
U8 = mybir.dt.uint8
AF = mybir.ActivationFunctionType
ALU = mybir.AluOpType
AX = mybir.AxisListType
EPS = 1e-6
SEQ = 4096
TT = 512
NT = SEQ // TT
NJ = NT // 2
NKB = SEQ // 128
MAGIC = 12582912.0
TWO_PI = 2.0 * math.pi


class Cfg:
    def __init__(self, D=4096, NH=16, G=64, DFF=11008):
        self.D, self.NH, self.G, self.DFF = D, NH, G, DFF
        self.DC = D // 128
        self.FW = NH * 128
        self.SW = G * 16
        self.SC = self.SW // 128
        self.NGP = G // 2
        self.KF = DFF // 128
        self.COL_K = self.FW
        self.COL_V = 2 * self.FW
        self.COL_F = 3 * self.FW
        self.COL_S5 = 3 * self.FW + NH
        self.COL_GF = self.COL_S5 + self.SW
        self.COL_GS = self.COL_GF + D
        self.INC = self.COL_GS + D


class Sched:
    def __init__(self, nc):
        self.nc = nc
        self.eng = {"pe": nc.tensor, "act": nc.scalar, "dve": nc.vector,
                    "pool": nc.gpsimd, "sp": nc.sync}
        self.sems = {}
        self.count = {}
        for n in ("pe", "act", "dve", "pool"):
            self.sems[n] = nc.alloc_semaphore("s_" + n)
            self.count[n] = 0
        self.rings = {"sp": [], "pool": []}
        for q, n in (("sp", 40), ("pool", 12)):
            for i in range(n):
                nm = f"dq_{q}{i}"
                self.sems[nm] = nc.alloc_semaphore("s_" + nm)
                self.count[nm] = 0
                self.rings[q].append(nm)
        self.dma_idx = {"sp": 0, "pool": 0}
        self.waited = {e: {} for e in self.eng}
        self.last_w = {}
        self.readers = {}
        self.n_wait = 0
        self.n_ops = 0

    def _need(self, eng, reads, writes, is_dma):
        need = {}

        def add(tok, raw):
            if tok is None:
                return
            s, v, ex = tok
            if (not is_dma) and ex == eng:
                if not raw or eng == "pe":
                    return
            if self.waited[eng].get(s, 0) >= v:
                return
            if need.get(s, 0) < v:
                need[s] = v
        for r in reads:
            add(self.last_w.get(r), True)
        for w in writes:
            add(self.last_w.get(w), False)
            for s, (v, ex) in self.readers.get(w, {}).items():
                add((s, v, ex), False)
        return need

    def _emit_waits(self, eng, need):
        e = self.eng[eng]
        for s, v in need.items():
            e.wait_ge(self.sems[s], v)
            self.waited[eng][s] = v
            self.n_wait += 1

    def _record(self, tok, reads, writes):
        s, v, ex = tok
        for w in writes:
            self.last_w[w] = tok
            self.readers[w] = {}
        for r in reads:
            self.readers.setdefault(r, {})[s] = (v, ex)

    def op(self, eng, fn, reads=(), writes=(), inc=True):
        self._emit_waits(eng, self._need(eng, reads, writes, False))
        ins = fn(self.eng[eng])
        self.n_ops += 1
        if inc:
            self.count[eng] += 1
            ins.then_inc(self.sems[eng], 1)
            tok = (eng, self.count[eng], eng)
        else:
            tok = (eng, self.count[eng] + 1, eng)
        self._record(tok, reads, writes)
        return ins

    def dma(self, q, out, in_, reads=(), writes=()):
        ring = self.rings[q]
        sname = ring[self.dma_idx[q] % len(ring)]
        self.dma_idx[q] += 1
        need = self._need(q, reads, writes, True)
        prev = self.count[sname]
        if prev > 0 and self.waited[q].get(sname, 0) < prev:
            need[sname] = max(need.get(sname, 0), prev)
        self._emit_waits(q, need)
        ins = self.eng[q].dma_start(out=out, in_=in_)
        self.count[sname] += 16
        ins.then_inc(self.sems[sname], 16)
        self._record((sname, self.count[sname], "dma"), reads, writes)
        self.n_ops += 1
        return ins

    def barrier(self, engines=None):
        for en in (engines or list(self.eng)):
            e = self.eng[en]
            for s, c in self.count.items():
                if c > 0 and self.waited[en].get(s, 0) < c:
                    e.wait_ge(self.sems[s], c)
                    self.waited[en][s] = c
                    self.n_wait += 1


class Arena:
    def __init__(self, nc, nbytes):
        self.t = nc.alloc_sbuf_tensor("arena", [128, nbytes], U8)
        self.nbytes = nbytes
        self.off = 0
        self.peak = 0

    def mark(self):
        return self.off

    def reset(self, m):
        self.off = m

    def alloc(self, shape, dt):
        esz = 4 if dt == F32 else 2
        n = esz
        for s in shape[1:]:
            n *= s
        self.off = (self.off + 63) // 64 * 64
        assert self.off + n <= self.nbytes, ("arena overflow", self.off, n, self.nbytes)
        ap = self.t[:, self.off:self.off + n].bitcast(dt)
        self.off += n
        self.peak = max(self.peak, self.off)
        if len(shape) == 3:
            ap = ap.rearrange("p (a b) -> p a b", a=shape[1])
        elif len(shape) == 4:
            ap = ap.rearrange("p (a b c) -> p a b c", a=shape[1], b=shape[2])
        return ap


class _Stop(Exception):
    pass


def build(cfg, stop=None):
    try:
        return _build(cfg, stop)
    except _Stop as ex:
        nc, S, A = ex.args
        S.barrier()
        return nc, dict(ops=S.n_ops, waits=S.n_wait, peak=A.peak, stopped=stop)


def _build(cfg, stop):
    D, NH, G, DFF = cfg.D, cfg.NH, cfg.G, cfg.DFF
    DC, FW, SW, SC, NGP, KF = cfg.DC, cfg.FW, cfg.SW, cfg.SC, cfg.NGP, cfg.KF
    nc = bass.Bass("TRN2", target_bir_lowering=False)

    def din(name, shape, dt=F32):
        return nc.dram_tensor(name, list(shape), dt, kind="ExternalInput").ap()

    xs = din("xs", [SEQ, D])
    xo = din("xo", [NJ * TT, D])
    w_in = din("w_in", [D, cfg.INC])
    w_glu = din("w_glu", [SW, SW])
    w_pf = din("w_pf", [FW, D])
    w_ps = din("w_ps", [SW, D])
    w_out = din("w_out", [D, D])
    w_gu = din("w_gu", [D, 2 * DFF])
    w_dn = din("w_dn", [DFF, D])
    NSM = 3 * DC + 2 * DC + 2 + NH + 8 + NKB + 2 * SC
    small_d = din("small", [128, NSM])
    qpos_d = din("qpos", [128, NJ * TT])
    consts_d = din("consts", [128, 4 * 128])
    oneh_d = din("oneh", [128, NH * 128], BF16)
    s5p_d = din("s5p", [128, 3 * NGP + 4 * NGP * 16])
    y_d = nc.dram_tensor("y", [NJ * TT, D], F32, kind="ExternalOutput").ap()
    kT_d = nc.dram_tensor("kT_s", [NH, 128, SEQ], BF16, kind="Internal").ap()
    vS_d = nc.dram_tensor("vS_s", [NH, 128, NKB, 128], BF16, kind="Internal").ap()
    yS_d = nc.dram_tensor("yS_s", [NJ, 128, SC, TT], BF16, kind="Internal").ap()
    hS_d = nc.dram_tensor("hS_s", [NJ * TT, D], F32, kind="Internal").ap()
    wtot = D * cfg.INC + SW * SW + FW * D + SW * D + D * D + D * 2 * DFF + DFF * D
    NSLOT = wtot // (128 * 8 * 512) + 64
    SPT = 192
    wc_list = [nc.dram_tensor(f"wc_s{i}", [SPT, 128, 8 * 512], BF16, kind="Internal").ap()
               for i in range((NSLOT + SPT - 1) // SPT)]

    class _WC:
        def __getitem__(self, idx):
            slot = idx[0]
            return wc_list[slot // SPT][(slot % SPT,) + tuple(idx[1:])]
    wc_d = _WC()

    S = Sched(nc)
    A = Arena(nc, 205 * 1024)

    def chk(name):
        if stop == name:
            raise _Stop(nc, S, A)
    ps = [nc.alloc_psum_tensor(f"ps{i}", [128, 512], F32) for i in range(8)]

    small = A.alloc([128, NSM], F32)
    consts = A.alloc([128, 512], F32)
    oneh = A.alloc([128, NH, 128], BF16)
    ident_f = consts[:, 0:128]
    tri_f = consts[:, 128:256]
    iota1 = consts[:, 256:384]
    ident_b = A.alloc([128, 128], BF16)
    ones_b = A.alloc([128, 128], BF16)
    ones_f = A.alloc([128, 128], F32)
    qn_s = A.alloc([128, 1], F32)
    Fall = A.alloc([128, NKB, NH], F32)
    carry = A.alloc([128, NH], F32)
    Fref = A.alloc([128, NJ, NH], F32)
    stat = A.alloc([128, 64], F32)
    eps_col = stat[:, 32:33]
    o = 0
    gmixT = small[:, o:o + DC]; o += DC
    gffnT = small[:, o:o + DC]; o += DC
    o += DC
    bgT = small[:, o:o + 2 * DC]; o += 2 * DC
    qn_c = small[:, o:o + 1]; o += 1
    kn_c = small[:, o:o + 1]; o += 1
    bf_rep = small[:, o:o + NH]; o += NH
    sel = small[:, o:o + 4]; o += 4
    nsel = small[:, o:o + 4]; o += 4
    kpos = small[:, o:o + NKB]; o += NKB
    d_l = small[:, o:o + SC]; o += SC
    bglu_l = small[:, o:o + SC]; o += SC
    assert o == NSM
    S.dma("sp", small, small_d, writes=["small"])
    S.dma("sp", consts, consts_d, writes=["consts"])
    S.dma("sp", oneh.rearrange("p a b -> p (a b)"), oneh_d, writes=["oneh"])
    S.op("dve", lambda e: e.tensor_copy(ident_b, ident_f), reads=["consts"], writes=["ident_b"])
    S.op("dve", lambda e: e.memset(ones_b, 1.0), writes=["ones_b"])
    S.op("dve", lambda e: e.memset(ones_f, 1.0), writes=["ones_f"])
    S.op("dve", lambda e: e.memset(carry, 0.0), writes=["carry"])
    S.op("dve", lambda e: e.memset(eps_col, EPS), writes=["eps_col"])
    S.op("dve", lambda e: e.tensor_scalar(qn_s, qn_c, 1.0 / math.sqrt(128.0), None, ALU.mult),
         reads=["small"], writes=["qn_s"])

    uT = A.alloc([128, DC, TT], BF16)
    NWT = 3
    KG = 8
    wt = [A.alloc([128, KG, 512], BF16) for _ in range(NWT)]
    wt_i = [0]
    m_base = A.mark()

    wc_slots = {}
    nfirst = [0]

    def load_w(w_ap, k0, kg, c0, ncl):
        b = wt_i[0]
        wt_i[0] = (b + 1) % NWT
        key = (w_ap.name, k0, c0, ncl)
        dst = wt[b][:, 0:kg, 0:ncl]
        if key in wc_slots:
            slot = wc_slots[key]
            S.dma("pool", dst, wc_d[slot, :, 0:kg * ncl].rearrange("p (k n) -> p k n", k=kg),
                  reads=[("wc", slot)], writes=[("wt", b)])
        else:
            src = w_ap[k0 * 128:(k0 + kg) * 128, c0:c0 + ncl].rearrange("(kc p) n -> p kc n", p=128)
            S.dma("pool", dst, src, writes=[("wt", b)])
            nfirst[0] += 1
            seqw = (w_ap.name == "w_in" and cfg.COL_K <= c0 < cfg.COL_GF)
            if seqw or nfirst[0] % 2 == 0:
                slot = len(wc_slots)
                assert slot < NSLOT
                wc_slots[key] = slot
                S.dma("sp", wc_d[slot, :, 0:kg * ncl].rearrange("p (k n) -> p k n", k=kg), dst,
                      reads=[("wt", b)], writes=[("wc", slot)])
        return b

    def gemm_fm(actT, ares, KC, w_ap, c0, ncols, epi, gcols=512, bank_sets=((0, 1, 2, 3), (4, 5, 6, 7)),
                gi0=0):
        gi = gi0
        for cg0 in range(c0, c0 + ncols, gcols):
            ncg = min(gcols, c0 + ncols - cg0)
            nct = ncg // 128
            banks = bank_sets[gi % len(bank_sets)]
            gi += 1
            for k0 in range(0, KC, KG):
                kg = min(KG, KC - k0)
                b = load_w(w_ap, k0, kg, cg0, ncg)
                for ct in range(nct):
                    for kc in range(kg):
                        S.op("pe", lambda e: e.matmul(ps[banks[ct]][:, :], wt[b][:, kc, ct * 128:(ct + 1) * 128],
                                                      actT[:, k0 + kc, :], start=(k0 + kc == 0),
                                                      stop=(k0 + kc == KC - 1)),
                             reads=[("wt", b), (ares, k0 + kc)], writes=[("ps", banks[ct])], inc=(kc == kg - 1))
                if tick[0] is not None and tick_on[0]:
                    tick[0]()
            for ct in range(nct):
                epi((cg0 - c0) // 128 + ct, banks[ct])
        return gi

    def gemm_tm(actT, ares, KC, w_ap, c0, ncols, epi, bank_sets=((0, 1, 2, 3), (4, 5, 6, 7))):
        gi = 0
        for cb0 in range(c0, c0 + ncols, 512):
            ncb = min(512, c0 + ncols - cb0)
            banks = bank_sets[gi % len(bank_sets)]
            for k0 in range(0, KC, KG):
                kg = min(KG, KC - k0)
                b = load_w(w_ap, k0, kg, cb0, ncb)
                for tb in range(4):
                    for kc in range(kg):
                        S.op("pe", lambda e: e.matmul(ps[banks[tb]][:, 0:ncb], actT[:, k0 + kc, tb * 128:(tb + 1) * 128],
                                                      wt[b][:, kc, 0:ncb], start=(k0 + kc == 0),
                                                      stop=(k0 + kc == KC - 1)),
                             reads=[("wt", b), (ares, k0 + kc)], writes=[("ps", banks[tb])], inc=(kc == kg - 1))
                if tick[0] is not None and tick_on[0]:
                    tick[0]()
            for tb in range(4):
                epi(gi, tb, banks[tb], ncb)
            gi += 1

    import os as _os
    DBG = _os.environ.get("KDBG", "")
    tr_i = [0]
    TRB = [6, 7]
    NRB = [6, 7]
    TR_ALL_ACT = [False]
    tick = [None]
    tick_on = [True]

    def transpose_to_uT(src, sres, tb, gT, gres):
        for c8 in range(0, DC, 8):
            n8 = min(8, DC - c8)
            bank = TRB[tr_i[0] % 2]
            tr_i[0] += 1
            pb = ps[bank][:, :].bitcast(BF16)
            for i in range(n8):
                S.op("pe", lambda e: e.transpose(pb[:, i * 128:(i + 1) * 128], src[:, (c8 + i) * 128:(c8 + i + 1) * 128],
                                                 ident_b),
                     reads=[sres, "ident_b"], writes=[("ps", bank)], inc=(i == n8 - 1))
            for i in range(n8):
                c = c8 + i
                dst = uT[:, c, tb * 128:(tb + 1) * 128]
                if "e" in DBG:
                    continue
                if "g" in DBG:
                    S.op("dve", lambda e: e.tensor_copy(dst, pb[:, i * 128:(i + 1) * 128]),
                         reads=[("ps", bank), gres], writes=[("uT", c)])
                    continue
                if bank == TRB[0] or TR_ALL_ACT[0]:
                    S.op("act", lambda e: e.activation(dst, pb[:, i * 128:(i + 1) * 128], AF.Copy, scale=gT[:, c:c + 1]),
                         reads=[("ps", bank), gres], writes=[("uT", c)])
                else:
                    S.op("dve", lambda e: e.tensor_scalar(dst, pb[:, i * 128:(i + 1) * 128], gT[:, c:c + 1], None, ALU.mult),
                         reads=[("ps", bank), gres], writes=[("uT", c)])

    def rstd_from(col, scale):
        S.op("dve", lambda e: e.tensor_scalar(col, col, scale, EPS, ALU.mult, ALU.add), reads=["rs"], writes=["rs"])
        S.op("act", lambda e: e.activation(col, col, AF.Sqrt), reads=["rs"], writes=["rs"])
        S.op("dve", lambda e: e.reciprocal(col, col), reads=["rs"], writes=["rs"])

    def norm1(x_d, r0, xst, xn):
        ssq = stat[:, 0:1]
        for tb in range(4):
            S.dma("sp", xst, x_d[r0 + tb * 128:r0 + (tb + 1) * 128, :], writes=["xst"])
            S.op("dve", lambda e: e.memset(ssq, 0.0), reads=["rs"], writes=["rs"])
            if "a" not in DBG:
                S.op("act", lambda e: e.activation(xn, xst, AF.Square, accum_out=ssq), reads=["xst", "rs"],
                     writes=["xn", "rs"])
            if "b" not in DBG:
                rstd_from(ssq, 1.0 / D)
            if "c" not in DBG:
                S.op("act", lambda e: e.activation(xn, xst, AF.Copy, scale=ssq), reads=["xst", "rs"], writes=["xn"])
            if "d" not in DBG:
                transpose_to_uT(xn, "xn", tb, gmixT, "small")

    nrm_i = [0]

    def qknorm(bank, gain_col, gres, out_ap, out_res, tmp):
        qraw, sq, rstd = tmp
        nb = NRB[nrm_i[0] % len(NRB)]
        nrm_i[0] += 1
        S.op("act", lambda e: e.activation(qraw, ps[bank][:, :], AF.Copy), reads=[("ps", bank)], writes=["qraw"])
        S.op("act", lambda e: e.activation(sq, ps[bank][:, :], AF.Square), reads=[("ps", bank)], writes=["sq"])
        S.op("pe", lambda e: e.matmul(ps[nb][:, :], ones_b, sq, start=True, stop=True),
             reads=["ones_b", "sq"], writes=[("ps", nb)])
        S.op("act", lambda e: e.activation(rstd, ps[nb][:, :], AF.Ln, bias=eps_col, scale=1.0 / 128.0),
             reads=[("ps", nb), "eps_col"], writes=["rstd"])
        S.op("act", lambda e: e.activation(rstd, rstd, AF.Exp, scale=-0.5), reads=["rstd"], writes=["rstd"])
        S.op("dve", lambda e: e.scalar_tensor_tensor(out_ap, qraw, gain_col, rstd, ALU.mult, ALU.mult),
             reads=["qraw", "rstd", gres], writes=[out_res])

    def range_reduce(t):
        tmp = rr_tmp[:, 0:t.shape[-1]] if len(t.shape) == 2 else None
        S.op("dve", lambda e: e.tensor_scalar(tmp, t, 1.0 / TWO_PI, MAGIC, ALU.mult, ALU.add), reads=["s5c"], writes=["s5c"])
        S.op("dve", lambda e: e.tensor_scalar(tmp, tmp, MAGIC, -TWO_PI, ALU.subtract, ALU.mult), reads=["s5c"], writes=["s5c"])
        S.op("dve", lambda e: e.tensor_tensor(t, t, tmp, ALU.add), reads=["s5c"], writes=["s5c"])

    xst = A.alloc([128, D], F32)
    xn = A.alloc([128, D], BF16)
    Blhs = A.alloc([128, NGP, 2, 128], BF16)
    Clhs = A.alloc([128, NGP, 2, 128], BF16)
    ctab = A.alloc([128, NGP, 128], BF16)
    stab = A.alloc([128, NGP, 128], BF16)
    sc = A.alloc([128, 16, NGP], F32)
    hst = A.alloc([128, NGP, 2], F32)
    m_setup = A.mark()
    s5p = A.alloc([128, 3 * NGP + 4 * NGP * 16], F32)
    rr_tmp = A.alloc([128, 128], F32)
    ang_t = A.alloc([128, 128], F32)
    zt = A.alloc([128, 128], F32)
    bb = A.alloc([128, 2, NGP, 16], F32)

    S.dma("sp", s5p, s5p_d, writes=["s5p"])
    lr = s5p[:, 0:NGP]
    li = s5p[:, NGP:2 * NGP]
    lsx = s5p[:, 2 * NGP:3 * NGP]
    o = 3 * NGP
    bre = s5p[:, o:o + NGP * 16].rearrange("p (a b) -> p a b", a=NGP); o += NGP * 16
    bim = s5p[:, o:o + NGP * 16].rearrange("p (a b) -> p a b", a=NGP); o += NGP * 16
    cre = s5p[:, o:o + NGP * 16].rearrange("p (a b) -> p a b", a=NGP); o += NGP * 16
    cim = s5p[:, o:o + NGP * 16].rearrange("p (a b) -> p a b", a=NGP); o += NGP * 16
    dt_, mag, ang, lbr, lbi, den, nre, facr, faci, c128, s128, tA, tB, tC = [sc[:, i, :] for i in range(14)]

    def s5op(eng, fn):
        S.op(eng, fn, reads=["s5p", "s5c"], writes=["s5c"])
    s5op("act", lambda e: e.activation(dt_, lsx, AF.Exp))
    s5op("dve", lambda e: e.tensor_tensor(tA, lr, dt_, ALU.mult))
    s5op("act", lambda e: e.activation(mag, tA, AF.Exp))
    s5op("dve", lambda e: e.tensor_tensor(ang, li, dt_, ALU.mult))
    s5op("dve", lambda e: e.tensor_copy(tA, ang))
    range_reduce(tA)
    s5op("act", lambda e: e.activation(tB, tA, AF.Sin))
    s5op("dve", lambda e: e.tensor_scalar(tA, ang, math.pi / 2, None, ALU.add))
    range_reduce(tA)
    s5op("act", lambda e: e.activation(tC, tA, AF.Sin))
    s5op("dve", lambda e: e.tensor_tensor(lbr, mag, tC, ALU.mult))
    s5op("dve", lambda e: e.tensor_tensor(lbi, mag, tB, ALU.mult))
    s5op("dve", lambda e: e.tensor_scalar(tA, ang, 128.0, None, ALU.mult))
    range_reduce(tA)
    s5op("act", lambda e: e.activation(s128, tA, AF.Sin))
    s5op("dve", lambda e: e.tensor_scalar(tA, ang, 128.0, math.pi / 2, ALU.mult, ALU.add))
    range_reduce(tA)
    s5op("act", lambda e: e.activation(c128, tA, AF.Sin))
    s5op("dve", lambda e: e.tensor_tensor(den, lr, lr, ALU.mult))
    s5op("dve", lambda e: e.tensor_tensor(tA, li, li, ALU.mult))
    s5op("dve", lambda e: e.tensor_tensor(den, den, tA, ALU.add))
    s5op("dve", lambda e: e.reciprocal(den, den))
    s5op("dve", lambda e: e.tensor_scalar(nre, lbr, -1.0, None, ALU.add))
    s5op("dve", lambda e: e.tensor_tensor(tA, nre, lr, ALU.mult))
    s5op("dve", lambda e: e.tensor_tensor(tB, lbi, li, ALU.mult))
    s5op("dve", lambda e: e.tensor_tensor(tA, tA, tB, ALU.add))
    s5op("dve", lambda e: e.tensor_tensor(facr, tA, den, ALU.mult))
    s5op("dve", lambda e: e.tensor_tensor(tA, lbi, lr, ALU.mult))
    s5op("dve", lambda e: e.tensor_tensor(tB, nre, li, ALU.mult))
    s5op("dve", lambda e: e.tensor_tensor(tA, tA, tB, ALU.subtract))
    s5op("dve", lambda e: e.tensor_tensor(faci, tA, den, ALU.mult))
    s5op("dve", lambda e: e.memset(hst, 0.0))
    s5op("dve", lambda e: e.memset(Blhs, 0.0))
    s5op("dve", lambda e: e.memset(Clhs, 0.0))
    for gp in range(NGP):
        q4 = gp % 4
        s5op("dve", lambda e: e.tensor_scalar(bb[:, 0, gp, :], bim[:, gp, :], faci[:, gp:gp + 1], None, ALU.mult))
        s5op("dve", lambda e: e.scalar_tensor_tensor(bb[:, 0, gp, :], bre[:, gp, :], facr[:, gp:gp + 1], bb[:, 0, gp, :],
                                                     ALU.mult, ALU.subtract))
        s5op("dve", lambda e: e.tensor_scalar(bb[:, 1, gp, :], bre[:, gp, :], faci[:, gp:gp + 1], None, ALU.mult))
        s5op("dve", lambda e: e.scalar_tensor_tensor(bb[:, 1, gp, :], bim[:, gp, :], facr[:, gp:gp + 1], bb[:, 1, gp, :],
                                                     ALU.mult, ALU.add))
        for ri in range(2):
            s5op("dve", lambda e: e.memset(zt, 0.0))
            s5op("dve", lambda e: e.tensor_copy(zt[0:64, q4 * 32:q4 * 32 + 16], bb[0:64, ri, gp, :]))
            s5op("dve", lambda e: e.tensor_copy(zt[64:128, q4 * 32 + 16:q4 * 32 + 32], bb[64:128, ri, gp, :]))
            S.op("pe", lambda e: e.transpose(ps[0][:, 0:128], zt, ident_f), reads=["s5c", "consts"], writes=[("ps", 0)])
            S.op("act", lambda e: e.activation(Blhs[:, gp, ri, :], ps[0][:, 0:128], AF.Copy), reads=[("ps", 0)],
                 writes=["s5c"])
        s5op("dve", lambda e: e.tensor_copy(Clhs[0:64, gp, 0, q4 * 32:q4 * 32 + 16], cre[0:64, gp, :]))
        s5op("dve", lambda e: e.tensor_copy(Clhs[64:128, gp, 0, q4 * 32 + 16:q4 * 32 + 32], cre[64:128, gp, :]))
        s5op("dve", lambda e: e.tensor_scalar(Clhs[0:64, gp, 1, q4 * 32:q4 * 32 + 16], cim[0:64, gp, :], -1.0, None, ALU.mult))
        s5op("dve", lambda e: e.tensor_scalar(Clhs[64:128, gp, 1, q4 * 32 + 16:q4 * 32 + 32], cim[64:128, gp, :], -1.0, None,
                                              ALU.mult))
        s5op("dve", lambda e: e.tensor_scalar(ang_t, iota1, ang[:, gp:gp + 1], None, ALU.mult))
        range_reduce(ang_t)
        s5op("act", lambda e: e.activation(stab[:, gp, :], ang_t, AF.Sin))
        s5op("dve", lambda e: e.tensor_scalar(ang_t, iota1, ang[:, gp:gp + 1], math.pi / 2, ALU.mult, ALU.add))
        range_reduce(ang_t)
        s5op("act", lambda e: e.activation(ctab[:, gp, :], ang_t, AF.Sin))

    S.barrier()
    A.reset(m_setup)
    s5u2 = [A.alloc([128, SC, TT], BF16)]

    class _TS:
        pass
    tsets = []
    for _k in range(2):
        T_ = _TS()
        for nm in ("bur", "bui", "t1", "t2", "t3", "t4", "wre", "wim"):
            setattr(T_, nm, A.alloc([128, TT], F32))
        T_.cini = A.alloc([128, 8], F32)
        tsets.append(T_)
    xr_b2 = [A.alloc([128, TT], BF16) for _ in range(2)]
    xi_b2 = [A.alloc([128, TT], BF16) for _ in range(2)]
    ysel = A.alloc([128, SC, TT], BF16)
    yt = A.alloc([128, TT], F32)
    qraw = A.alloc([128, TT], F32)
    sqb = A.alloc([128, TT], BF16)
    rstd = A.alloc([128, TT], F32)
    kout = [A.alloc([128, TT], BF16) for _ in range(2)]
    vstg = [A.alloc([128, 512], BF16) for _ in range(2)]
    ftmp = A.alloc([128, 4, NH], F32)

    chk('setup')

    def bc4(tab_gp):
        return tab_gp.unsqueeze(1).broadcast_to([128, 4, 128])

    def v4(ap):
        return ap.rearrange("p (a b) -> p a b", a=4)

    def s5_front_gen(t, gp, j):
        k = gp % 2
        T = tsets[k]
        c = gp // 4
        bk = (0, 1) if k == 0 else (2, 3)
        s5u = s5u2[0]
        sres = ("s5u", 0, c)
        xr_b, xi_b = xr_b2[k], xi_b2[k]

        def R(n):
            return (n, k)
        S.op("pe", lambda e: e.matmul(ps[bk[0]][:, :], Blhs[:, gp, 0, :], s5u[:, c, :], start=True, stop=True),
             reads=["s5c", sres], writes=[("ps", bk[0])])
        S.op("pe", lambda e: e.matmul(ps[bk[1]][:, :], Blhs[:, gp, 1, :], s5u[:, c, :], start=True, stop=True),
             reads=["s5c", sres], writes=[("ps", bk[1])])
        S.op("act", lambda e: e.activation(T.bur, ps[bk[0]][:, :], AF.Copy), reads=[("ps", bk[0])], writes=[R("bur")])
        S.op("act", lambda e: e.activation(T.bui, ps[bk[1]][:, :], AF.Copy), reads=[("ps", bk[1])], writes=[R("bui")])
        yield
        ct4, st4 = bc4(ctab[:, gp, :]), bc4(stab[:, gp, :])
        S.op("dve", lambda e: e.tensor_tensor(v4(T.t1), v4(T.bur), ct4, ALU.mult), reads=[R("bur"), "s5c"], writes=[R("t1")])
        S.op("pool", lambda e: e.tensor_tensor(v4(T.t3), v4(T.bui), ct4, ALU.mult), reads=[R("bui"), "s5c"], writes=[R("t3")])
        yield
        S.op("dve", lambda e: e.tensor_tensor(v4(T.t2), v4(T.bui), st4, ALU.mult), reads=[R("bui"), "s5c"], writes=[R("t2")])
        S.op("pool", lambda e: e.tensor_tensor(v4(T.t4), v4(T.bur), st4, ALU.mult), reads=[R("bur"), "s5c"], writes=[R("t4")])
        yield
        S.op("dve", lambda e: e.tensor_tensor(T.t1, T.t1, T.t2, ALU.add), reads=[R("t1"), R("t2")], writes=[R("t1")])
        S.op("pool", lambda e: e.tensor_tensor(T.t3, T.t3, T.t4, ALU.subtract), reads=[R("t3"), R("t4")], writes=[R("t3")])
        yield "mid"
        rb = mag[:, gp:gp + 1].broadcast_to([128, 128])
        cini = T.cini
        for sb in range(4):
            ir = hst[:, gp, 0:1] if sb == 0 else cini[:, 0:1]
            ii = hst[:, gp, 1:2] if sb == 0 else cini[:, 1:2]
            sl = slice(sb * 128, (sb + 1) * 128)
            S.op("dve", lambda e: e.tensor_tensor_scan(T.wre[:, sl], rb, T.t1[:, sl], ir, ALU.mult, ALU.add),
                 reads=[R("t1"), "s5c", R("cini"), ("hst", gp)], writes=[R("wre")])
            S.op("dve", lambda e: e.tensor_tensor_scan(T.wim[:, sl], rb, T.t3[:, sl], ii, ALU.mult, ALU.add),
                 reads=[R("t3"), "s5c", R("cini"), ("hst", gp)], writes=[R("wim")])
            yield
            lr_, li_ = T.wre[:, sb * 128 + 127:sb * 128 + 128], T.wim[:, sb * 128 + 127:sb * 128 + 128]
            dr = hst[:, gp, 0:1] if sb == 3 else cini[:, 0:1]
            di = hst[:, gp, 1:2] if sb == 3 else cini[:, 1:2]
            dres = ("hst", gp) if sb == 3 else R("cini")
            S.op("dve", lambda e: e.tensor_scalar(cini[:, 2:3], li_, s128[:, gp:gp + 1], None, ALU.mult),
                 reads=[R("wim"), "s5c"], writes=[R("cini2")])
            S.op("dve", lambda e: e.tensor_scalar(cini[:, 3:4], lr_, s128[:, gp:gp + 1], None, ALU.mult),
                 reads=[R("wre"), "s5c"], writes=[R("cini2")])
            yield
            S.op("dve", lambda e: e.scalar_tensor_tensor(dr, lr_, c128[:, gp:gp + 1], cini[:, 2:3], ALU.mult, ALU.subtract),
                 reads=[R("wre"), R("cini2"), "s5c"], writes=[dres])
            S.op("dve", lambda e: e.scalar_tensor_tensor(di, li_, c128[:, gp:gp + 1], cini[:, 3:4], ALU.mult, ALU.add),
                 reads=[R("wim"), R("cini2"), "s5c"], writes=[dres])
            yield
        S.op("pool", lambda e: e.tensor_tensor(v4(T.bur), v4(T.wre), ct4, ALU.mult), reads=[R("wre"), "s5c"], writes=[R("bur")])
        S.op("dve", lambda e: e.tensor_tensor(v4(T.t2), v4(T.wim), st4, ALU.mult), reads=[R("wim"), "s5c"], writes=[R("t2")])
        yield
        S.op("pool", lambda e: e.tensor_tensor(v4(T.bui), v4(T.wim), ct4, ALU.mult), reads=[R("wim"), "s5c"], writes=[R("bui")])
        S.op("dve", lambda e: e.tensor_tensor(v4(T.t4), v4(T.wre), st4, ALU.mult), reads=[R("wre"), "s5c"], writes=[R("t4")])
        yield
        S.op("pool", lambda e: e.tensor_tensor(xr_b, T.bur, T.t2, ALU.subtract), reads=[R("bur"), R("t2")], writes=[("xr_b", k)])
        S.op("pool", lambda e: e.tensor_tensor(xi_b, T.bui, T.t4, ALU.add), reads=[R("bui"), R("t4")], writes=[("xi_b", k)])
        yield

    def s5_back(t, gp, j):
        c = gp // 4
        yb = 4 + (c % 2)
        s5u = s5u2[0]
        sres = ("s5u", 0, c)
        xr_b, xi_b = xr_b2[gp % 2], xi_b2[gp % 2]
        xrr, xir = ("xr_b", gp % 2), ("xi_b", gp % 2)
        q4 = gp % 4
        S.op("pe", lambda e: e.matmul(ps[yb][:, :], Clhs[:, gp, 0, :], xr_b, start=(q4 == 0), stop=False),
             reads=["s5c", xrr], writes=[("ps", yb)], inc=False)
        S.op("pe", lambda e: e.matmul(ps[yb][:, :], Clhs[:, gp, 1, :], xi_b, start=False, stop=(q4 == 3 or gp == NGP - 1)),
             reads=["s5c", xir], writes=[("ps", yb)])
        if q4 == 3 or gp == NGP - 1:
            S.op("dve", lambda e: e.scalar_tensor_tensor(yt, s5u[:, c, :], d_l[:, c:c + 1], ps[yb][:, :], ALU.mult, ALU.add),
                 reads=[sres, "small", ("ps", yb)], writes=["yt"])
            if t % 2 == 0:
                S.op("dve", lambda e: e.tensor_scalar(ysel[:, c, :], yt, sel[:, j:j + 1], None, ALU.mult),
                     reads=["yt", "small"], writes=[("ysel", c)])
            else:
                S.op("dve", lambda e: e.scalar_tensor_tensor(ysel[:, c, :], yt, nsel[:, j:j + 1], ysel[:, c, :], ALU.mult, ALU.add),
                     reads=["yt", "small", ("ysel", c)], writes=[("ysel", c)])

    def s5_gen(t):
        j = t // 2
        pending = []
        for g0 in range(0, NGP, 2):
            gens = [s5_front_gen(t, g0, j), s5_front_gen(t, g0 + 1, j)]
            alive = list(gens)
            n_mid = 0
            while alive:
                for g in list(alive):
                    try:
                        r = next(g)
                    except StopIteration:
                        alive.remove(g)
                        continue
                    if r == "mid":
                        n_mid += 1
                        if n_mid == 2:
                            for gp_ in pending:
                                s5_back(t, gp_, j)
                            pending = []
            pending = [g0, g0 + 1]
            yield
        for gp_ in pending:
            s5_back(t, gp_, j)
        if t % 2 == 1:
            S.dma("sp", yS_d[j], ysel, reads=[("ysel", c) for c in range(SC)], writes=[("yS", j)])

    def make_tick(gen):
        def _t():
            try:
                next(gen)
            except StopIteration:
                tick[0] = None
        return _t

    def drain(gen):
        for _ in gen:
            pass

    pend = [None]
    for t in range(NT):
        j = t // 2
        norm1(xs, t * TT, xst, xn)
        chk('seq_norm')
        ko_i = [0]

        def epi_k(ct, bank):
            kb_ = ko_i[0] % 2
            ko_i[0] += 1
            qknorm(bank, kn_c, "small", kout[kb_], ("kout", kb_), (qraw, sqb, rstd))
            S.dma("sp", kT_d[ct, :, t * TT:(t + 1) * TT], kout[kb_], reads=[("kout", kb_)], writes=[("kT", ct, t)])
        tick_on[0] = 'k' not in DBG
        gemm_fm(uT, "uT", DC, w_in, cfg.COL_K, FW, epi_k, gcols=256, bank_sets=((0, 1), (2, 3), (4, 5)))
        chk('seq_k')
        vs_i = [0]

        def epi_v(cb, tb, bank, ncb):
            kb = t * 4 + tb
            vb_ = vs_i[0] % 2
            vs_i[0] += 1
            nh_ = ncb // 128
            S.op("act", lambda e: e.activation(vstg[vb_][:, 0:ncb], ps[bank][:, 0:ncb], AF.Copy),
                 reads=[("ps", bank)], writes=[("vstg", vb_)])
            S.dma("sp", vS_d[cb * 4:cb * 4 + nh_, :, kb, :].rearrange("h p d -> p h d"),
                  vstg[vb_][:, 0:ncb].rearrange("p (h d) -> p h d", d=128),
                  reads=[("vstg", vb_)], writes=[("vS", cb, kb)])
        tick_on[0] = 'v' not in DBG
        gemm_tm(uT, "uT", DC, w_in, cfg.COL_V, FW, epi_v)

        chk('seq_v')
        def epi_f(cb, tb, bank, ncb):
            kb = t * 4 + tb
            fl = ftmp[:, 0, :]
            fe = ftmp[:, 1, :]
            lf = ftmp[:, 2, :]
            if t % 2 == 0 and tb == 0:
                S.op("dve", lambda e: e.tensor_copy(Fref[:, j, :], carry), reads=["carry"], writes=["Fref"])
            S.op("dve", lambda e: e.tensor_tensor(fl, ps[bank][:, 0:NH], bf_rep, ALU.add),
                 reads=[("ps", bank), "small"], writes=["fl"])
            S.op("act", lambda e: e.activation(fe, fl, AF.Exp, scale=-1.0), reads=["fl"], writes=["fe"])
            S.op("act", lambda e: e.activation(lf, fe, AF.Ln, bias=1.0), reads=["fe"], writes=["lf"])
            S.op("pe", lambda e: e.matmul(ps[bank][:, 128:128 + NH], tri_f, lf, start=True, stop=True),
                 reads=["consts", "lf"], writes=[("ps", bank)], inc=False)
            S.op("pe", lambda e: e.matmul(ps[bank][:, 256:256 + NH], ones_f, lf, start=True, stop=True),
                 reads=["ones_f", "lf"], writes=[("ps", bank)])
            S.op("dve", lambda e: e.tensor_tensor(Fall[:, kb, :], carry, ps[bank][:, 128:128 + NH], ALU.subtract),
                 reads=["carry", ("ps", bank)], writes=["Fall"])
            S.op("dve", lambda e: e.tensor_tensor(carry, carry, ps[bank][:, 256:256 + NH], ALU.subtract),
                 reads=["carry", ("ps", bank)], writes=["carry"])
        tick_on[0] = 'f' not in DBG
        gemm_tm(uT, "uT", DC, w_in, cfg.COL_F, NH, epi_f)
        chk('seq_f')

        def epi_s5(ct, bank):
            S.op("act", lambda e: e.activation(s5u2[0][:, ct, :], ps[bank][:, :], AF.Copy), reads=[("ps", bank)],
                 writes=[("s5u", 0, ct)])
        tick_on[0] = False
        gemm_fm(uT, "uT", DC, w_in, cfg.COL_S5, SW, epi_s5, bank_sets=((4, 5, 6, 7),))
        tick_on[0] = True
        chk('seq_s5in')
        if pend[0] is not None:
            drain(pend[0])
        pend[0] = s5_gen(t)
        tick[0] = make_tick(pend[0])
        tick[0] = None
        drain(pend[0])
        pend[0] = None
        chk('seq_s5')
    tick[0] = None
    S.barrier()
    seq_peak = A.peak
    chk('seq')

    TRB[:] = [6, 7]
    NRB[:] = [6, 7]
    TR_ALL_ACT[0] = False
    A.reset(m_base)
    attnT = A.alloc([128, NH, TT], BF16)
    ssmT = A.alloc([128, SC, TT], BF16)
    mergedT = A.alloc([128, DC, TT], BF16)
    m_own = A.mark()

    for j in range(NJ):
        nkb = (2 * j + 2) * 4
        A.reset(m_own)
        QT = A.alloc([128, NH, TT], BF16)
        m_a = A.mark()
        xst = A.alloc([128, D], F32)
        xn = A.alloc([128, D], BF16)
        qraw = A.alloc([128, TT], F32)
        sqb = A.alloc([128, TT], BF16)
        rstd = A.alloc([128, TT], F32)
        norm1(xo, j * TT, xst, xn)

        def epi_q(ct, bank):
            qknorm(bank, qn_s, "qn_s", QT[:, ct, :], ("QT", ct), (qraw, sqb, rstd))
        gemm_fm(uT, "uT", DC, w_in, 0, FW, epi_q, gcols=256, bank_sets=((0, 1), (2, 3), (4, 5)))
        S.barrier()
        chk('own_q')
        A.reset(m_a)
        kst = [A.alloc([128, SEQ], BF16) for _ in range(2)]
        vst = [A.alloc([128, NKB, 128], BF16) for _ in range(2)]
        maskadd = A.alloc([128, 8, TT], BF16)
        qposb = A.alloc([128, TT], F32)
        mk = A.alloc([128, TT], F32)
        biasK = A.alloc([128, NKB, NH], F32)
        fqt = A.alloc([128, 4, 48], F32)
        fqh = A.alloc([128, TT], BF16)
        fqhf = A.alloc([128, TT], F32)
        fql = A.alloc([128, TT], BF16)
        fqhl = A.alloc([128, TT], BF16)
        PT = [A.alloc([128, TT], BF16) for _ in range(3)]
        mt = [A.alloc([128, TT], F32) for _ in range(2)]
        rc = A.alloc([128, TT], F32)
        S.dma("sp", qposb, qpos_d[:, j * TT:(j + 1) * TT], writes=["qposb"])
        S.op("dve", lambda e: e.memset(fqt, 0.0), writes=["fqt"])
        S.op("dve", lambda e: e.memset(fqhl, 0.0), writes=["fqhl"])
        for tb in range(4):
            fa = Fall[:, (2 * j) * 4 + tb, :]
            fb = Fall[:, (2 * j + 1) * 4 + tb, :]
            S.op("dve", lambda e: e.tensor_scalar(fqt[:, tb, 0:NH], fa, sel[:, j:j + 1], None, ALU.mult),
                 reads=["Fall", "small"], writes=["fqt"])
            S.op("dve", lambda e: e.scalar_tensor_tensor(fqt[:, tb, 0:NH], fb, nsel[:, j:j + 1], fqt[:, tb, 0:NH],
                                                         ALU.mult, ALU.add), reads=["Fall", "small", "fqt"], writes=["fqt"])
            S.op("dve", lambda e: e.tensor_tensor(fqt[:, tb, 0:NH], fqt[:, tb, 0:NH], Fref[:, j, :], ALU.subtract),
                 reads=["fqt", "Fref"], writes=["fqt"])
            S.op("dve", lambda e: e.tensor_copy(fqt[:, tb, 32:32 + NH], fqt[:, tb, 0:NH]), reads=["fqt"], writes=["fqt"])
        for tb in range(4):
            S.op("pe", lambda e: e.transpose(ps[7][0:48, tb * 128:(tb + 1) * 128], fqt[:, tb, :], ident_f),
                 reads=["fqt", "consts"], writes=[("ps", 7)], inc=(tb == 3))
        S.op("dve", lambda e: e.tensor_copy(fqh[0:48, :], ps[7][0:48, :]), reads=[("ps", 7)], writes=["fqh"])
        S.op("dve", lambda e: e.tensor_copy(fqhf[0:48, :], fqh[0:48, :]), reads=["fqh"], writes=["fqhf"])
        S.op("dve", lambda e: e.tensor_tensor(fql[0:48, :], ps[7][0:48, :], fqhf[0:48, :], ALU.subtract),
             reads=[("ps", 7), "fqhf"], writes=["fql"])
        S.op("dve", lambda e: e.tensor_copy(fqhl[0:16, :], fqh[0:16, :]), reads=["fqh", "fqhl"], writes=["fqhl"])
        S.op("dve", lambda e: e.tensor_copy(fqhl[32:48, :], fql[32:48, :]), reads=["fql", "fqhl"], writes=["fqhl"])
        S.op("dve", lambda e: e.tensor_tensor(biasK[:, 0:nkb, :], Fall[:, 0:nkb, :],
                                              Fref[:, j, :].unsqueeze(1).broadcast_to([128, nkb, NH]), ALU.subtract),
             reads=["Fall", "Fref"], writes=["biasK"])
        S.op("dve", lambda e: e.tensor_scalar(biasK[:, 0:nkb, :], biasK[:, 0:nkb, :], -1.0, None, ALU.mult),
             reads=["biasK"], writes=["biasK"])
        for mi in range(8):
            kb = nkb - 8 + mi
            S.op("dve", lambda e: e.tensor_scalar(mk, qposb, kpos[:, kb:kb + 1], None, ALU.is_ge),
                 reads=["qposb", "small"], writes=["mk"])
            S.op("dve", lambda e: e.tensor_scalar(maskadd[:, mi, :], mk, -1.0, 30000.0, ALU.add, ALU.mult),
                 reads=["mk"], writes=["maskadd"])
        for h in range(NH):
            hb = h % 2
            S.dma("sp", kst[hb][:, 0:nkb * 128], kT_d[h, :, 0:nkb * 128],
                  reads=[("kT", h, tt_) for tt_ in range(2 * j + 2)], writes=[("kst", hb)])
            S.dma("sp", vst[hb][:, 0:nkb, :], vS_d[h, :, 0:nkb, :],
                  reads=[("vS", h // 4, kk) for kk in range(nkb)], writes=[("vst", hb)])
            bO, bD = 3 + hb, 5 + hb

            def s_step(kb):
                sb_ = kb % 3
                S.op("pe", lambda e: e.matmul(ps[sb_][:, :], kst[hb][:, kb * 128:(kb + 1) * 128], QT[:, h, :],
                                              start=True, stop=False),
                     reads=[("kst", hb), ("QT", h)], writes=[("ps", sb_)], inc=False)
                S.op("pe", lambda e: e.matmul(ps[sb_][:, :], oneh[0:48, h, :], fqhl[0:48, :], start=False, stop=True),
                     reads=["oneh", "fqhl"], writes=[("ps", sb_)])
                pt = PT[sb_]
                if kb >= nkb - 8:
                    mi = kb - (nkb - 8)
                    m_ = mt[kb % 2]
                    S.op("dve", lambda e: e.scalar_tensor_tensor(m_, ps[sb_][:, :], biasK[:, kb, h:h + 1], maskadd[:, mi, :],
                                                                 ALU.add, ALU.add),
                         reads=[("ps", sb_), "biasK", "maskadd"], writes=[("mt", kb % 2)])
                    S.op("act", lambda e: e.activation(pt, m_, AF.Exp), reads=[("mt", kb % 2)], writes=[("PT", sb_)])
                else:
                    S.op("act", lambda e: e.activation(pt, ps[sb_][:, :], AF.Exp, bias=biasK[:, kb, h:h + 1]),
                         reads=[("ps", sb_), "biasK"], writes=[("PT", sb_)])

            def pv_step(kb):
                sb_ = kb % 3
                S.op("pe", lambda e: e.matmul(ps[bO][:, :], vst[hb][:, kb, :], PT[sb_], start=(kb == 0), stop=(kb == nkb - 1)),
                     reads=[("vst", hb), ("PT", sb_)], writes=[("ps", bO)], inc=False)
                S.op("pe", lambda e: e.matmul(ps[bD][:, :], ones_b, PT[sb_], start=(kb == 0), stop=(kb == nkb - 1)),
                     reads=["ones_b", ("PT", sb_)], writes=[("ps", bD)])
            for i in range(nkb + 2):
                if i < nkb:
                    s_step(i)
                if i >= 2:
                    pv_step(i - 2)
            S.op("dve", lambda e: e.reciprocal(rc, ps[bD][:, :]), reads=[("ps", bD)], writes=["rc"])
            S.op("dve", lambda e: e.tensor_tensor(attnT[:, h, :], ps[bO][:, :], rc, ALU.mult),
                 reads=[("ps", bO), "rc"], writes=[("attnT", h)])
        S.barrier()
        chk('attn')
        A.reset(m_a)
        ysT = A.alloc([128, SC, TT], BF16)
        gT_ = A.alloc([128, SC, TT], BF16)
        g1 = A.alloc([128, TT], F32)
        g2 = A.alloc([128, TT], F32)
        sig = A.alloc([128, TT], F32)
        gfT = A.alloc([128, 4, TT], BF16)
        gsT = A.alloc([128, 4, TT], BF16)
        mtmp = A.alloc([128, 4, TT], F32)
        mt2 = A.alloc([128, TT], F32)
        S.dma("sp", ysT, yS_d[j], reads=[("yS", j)], writes=["ysT"])
        for c in range(SC):
            yc = ysT[:, c, :]
            S.op("dve", lambda e: e.tensor_tensor(g1, yc, yc, ALU.mult), reads=["ysT"], writes=["g1"])
            S.op("dve", lambda e: e.tensor_scalar(g1, g1, 0.044715, 1.0, ALU.mult, ALU.add), reads=["g1"], writes=["g1"])
            S.op("dve", lambda e: e.tensor_tensor(g1, g1, yc, ALU.mult), reads=["g1", "ysT"], writes=["g1"])
            S.op("act", lambda e: e.activation(g2, g1, AF.Sigmoid, scale=2.0 * math.sqrt(2.0 / math.pi)), reads=["g1"],
                 writes=["g2"])
            S.op("dve", lambda e: e.tensor_tensor(gT_[:, c, :], g2, yc, ALU.mult),
                 reads=["g2", "ysT"], writes=[("gT", c)])

        def epi_glu(ct, bank):
            S.op("act", lambda e: e.activation(sig, ps[bank][:, :], AF.Sigmoid, bias=bglu_l[:, ct:ct + 1]),
                 reads=[("ps", bank), "small"], writes=["sig"])
            S.op("dve", lambda e: e.tensor_tensor(ssmT[:, ct, :], gT_[:, ct, :], sig, ALU.mult),
                 reads=["sig", ("gT", ct)], writes=[("ssmT", ct)])
        gemm_fm(gT_, "gT", SC, w_glu, 0, SW, epi_glu)
        gi = 0
        for cg in range(0, D, 512):
            c0t = cg // 128

            def epi_gf(ct, bank):
                S.op("act", lambda e: e.activation(gfT[:, ct, :], ps[bank][:, :], AF.Sigmoid, bias=bgT[:, c0t + ct:c0t + ct + 1]),
                     reads=[("ps", bank), "small"], writes=[("gfT", ct)])

            def epi_gs(ct, bank):
                S.op("act", lambda e: e.activation(gsT[:, ct, :], ps[bank][:, :], AF.Sigmoid,
                                                   bias=bgT[:, DC + c0t + ct:DC + c0t + ct + 1]),
                     reads=[("ps", bank), "small"], writes=[("gsT", ct)])

            def epi_pf(ct, bank):
                S.op("dve", lambda e: e.tensor_tensor(mtmp[:, ct, :], ps[bank][:, :], gfT[:, ct, :], ALU.mult),
                     reads=[("ps", bank), ("gfT", ct)], writes=[("mtmp", ct)])

            def epi_ps(ct, bank):
                S.op("dve", lambda e: e.tensor_tensor(mt2, ps[bank][:, :], gsT[:, ct, :], ALU.mult),
                     reads=[("ps", bank), ("gsT", ct)], writes=["mt2"])
                S.op("pool", lambda e: e.tensor_tensor(mergedT[:, c0t + ct, :], mt2, mtmp[:, ct, :], ALU.add),
                     reads=["mt2", ("mtmp", ct)], writes=[("mergedT", c0t + ct)])
            gi = gemm_fm(uT, "uT", DC, w_in, cfg.COL_GF + cg, 512, epi_gf, gi0=gi)
            gi = gemm_fm(uT, "uT", DC, w_in, cfg.COL_GS + cg, 512, epi_gs, gi0=gi)
            gi = gemm_fm(attnT, "attnT", NH, w_pf, cg, 512, epi_pf, gi0=gi)
            gi = gemm_fm(ssmT, "ssmT", SC, w_ps, cg, 512, epi_ps, gi0=gi)
        S.barrier()
        chk('B')
        A.reset(m_a)
        xres = [A.alloc([128, 512], F32) for _ in range(2)]
        htmp = [A.alloc([128, 512], F32) for _ in range(2)]
        junk = A.alloc([128, 512], BF16)
        h16 = A.alloc([128, 4, D], BF16)
        ssq2 = A.alloc([128, 4, 8], F32)
        NCB = D // 512
        S.op("dve", lambda e: e.memset(ssq2, 0.0), writes=["ssq2"])
        xi_ = [0]

        def epi_wo(cb, tb, bank, ncb):
            b_ = xi_[0] % 2
            xi_[0] += 1
            r0 = j * TT + tb * 128
            S.dma("sp", xres[b_], xo[r0:r0 + 128, cb * 512:(cb + 1) * 512], writes=[("xres", b_)])
            S.op("dve", lambda e: e.tensor_tensor(htmp[b_], ps[bank][:, :], xres[b_], ALU.add),
                 reads=[("ps", bank), ("xres", b_)], writes=[("htmp", b_)])
            S.dma("sp", hS_d[r0:r0 + 128, cb * 512:(cb + 1) * 512], htmp[b_], reads=[("htmp", b_)],
                  writes=[("hS", tb, cb)])
            S.op("act", lambda e: e.activation(h16[:, tb, cb * 512:(cb + 1) * 512], htmp[b_], AF.Copy),
                 reads=[("htmp", b_)], writes=[("h16", tb)])
            S.op("act", lambda e: e.activation(junk, htmp[b_], AF.Square, accum_out=ssq2[:, tb, cb:cb + 1]),
                 reads=[("htmp", b_), "ssq2"], writes=["junk", "ssq2"])
        gemm_tm(mergedT, "mergedT", DC, w_out, 0, D, epi_wo)
        for tb in range(4):
            ssq = stat[:, 0:1]
            S.op("dve", lambda e: e.reduce_sum(ssq, ssq2[:, tb, 0:NCB], AX.X), reads=["ssq2", "rs"], writes=["rs"])
            rstd_from(ssq, 1.0 / D)
            S.op("act", lambda e: e.activation(h16[:, tb, :], h16[:, tb, :], AF.Copy, scale=ssq),
                 reads=[("h16", tb), "rs"], writes=[("h16", tb)])
            transpose_to_uT(h16[:, tb, :], ("h16", tb), tb, gffnT, "small")
        S.barrier()
        chk('C')
        A.reset(m_base)
        actT = A.alloc([128, KF, TT], BF16)
        sgT = A.alloc([128, 4, TT], BF16)
        hres = [A.alloc([128, 512], F32) for _ in range(2)]
        ost = [A.alloc([128, 512], F32) for _ in range(2)]
        gi = 0
        for cg in range(0, DFF, 512):
            ncg = min(512, DFF - cg)
            c0t = cg // 128

            def epi_g(ct, bank):
                S.op("act", lambda e: e.activation(sgT[:, ct, :], ps[bank][:, :], AF.Silu), reads=[("ps", bank)],
                     writes=[("sgT", ct)])

            def epi_u(ct, bank):
                S.op("dve", lambda e: e.tensor_tensor(actT[:, c0t + ct, :], ps[bank][:, :], sgT[:, ct, :], ALU.mult),
                     reads=[("ps", bank), ("sgT", ct)], writes=[("actT", c0t + ct)])
            gi = gemm_fm(uT, "uT", DC, w_gu, cg, ncg, epi_g, gi0=gi)
            gi = gemm_fm(uT, "uT", DC, w_gu, DFF + cg, ncg, epi_u, gi0=gi)
        oi_ = [0]

        def epi_dn(cb, tb, bank, ncb):
            b_ = oi_[0] % 2
            oi_[0] += 1
            r0 = j * TT + tb * 128
            S.dma("sp", hres[b_], hS_d[r0:r0 + 128, cb * 512:(cb + 1) * 512], reads=[("hS", tb, cb)],
                  writes=[("hres", b_)])
            S.op("dve", lambda e: e.tensor_tensor(ost[b_], ps[bank][:, :], hres[b_], ALU.add),
                 reads=[("ps", bank), ("hres", b_)], writes=[("ost", b_)])
            S.dma("sp", y_d[r0:r0 + 128, cb * 512:(cb + 1) * 512], ost[b_], reads=[("ost", b_)], writes=[("y", r0, cb)])
        gemm_tm(actT, "actT", KF, w_dn, 0, D, epi_dn)
        S.barrier()
    S.barrier()
    info = dict(ops=S.n_ops, waits=S.n_wait, seq_peak=seq_peak, peak=A.peak)
    return nc, info


OWN_TILES = ((0, 3, 4, 7), (1, 2, 5, 6))


def make_in_maps(cfg, inp, n_pairs):
    D, NH, G, DFF = cfg.D, cfg.NH, cfg.G, cfg.DFF
    DC, SC, NGP = cfg.DC, cfg.SC, cfg.NGP
    f32 = np.float32

    def colT(v, n):
        return np.ascontiguousarray(np.asarray(v, f32).reshape(n, 128).T)

    def rep(v):
        v = np.asarray(v, f32)
        return np.ascontiguousarray(np.broadcast_to(v[None, :], (128, v.shape[0])))
    consts = np.zeros((128, 512), f32)
    consts[:, 0:128] = np.eye(128, dtype=f32)
    consts[:, 128:256] = np.triu(np.ones((128, 128), f32))
    consts[:, 256:384] = np.arange(1, 129, dtype=f32)[None, :]
    oneh = np.zeros((128, NH, 128), f32)
    for h in range(NH):
        oneh[h, h, :] = 1.0
        oneh[32 + h, h, :] = 1.0
    oneh = oneh.reshape(128, NH * 128).astype(ml_dtypes.bfloat16)
    kpos = (np.arange(NKB, dtype=f32)[None, :] * 128 + np.arange(128, dtype=f32)[:, None])

    def pairl(a):
        return np.asarray(a, f32).reshape(NGP, 2, 64).transpose(1, 2, 0).reshape(128, NGP)
    lam_re, lam_im = inp["s5_lambda_re"][0], inp["s5_lambda_im"][0]
    ls = np.broadcast_to(np.asarray(inp["s5_log_step"][0], f32)[:, None], (G, 64))
    bre = np.asarray(inp["s5_b_re"][0], f32).reshape(NGP, 2, 64, 16).transpose(1, 2, 0, 3).reshape(128, NGP * 16)
    bim = np.asarray(inp["s5_b_im"][0], f32).reshape(NGP, 2, 64, 16).transpose(1, 2, 0, 3).reshape(128, NGP * 16)
    cre = np.asarray(inp["s5_c_re"][0], f32).reshape(NGP, 2, 16, 64).transpose(1, 3, 0, 2).reshape(128, NGP * 16)
    cim = np.asarray(inp["s5_c_im"][0], f32).reshape(NGP, 2, 16, 64).transpose(1, 3, 0, 2).reshape(128, NGP * 16)
    s5p = np.ascontiguousarray(np.concatenate([pairl(lam_re), pairl(lam_im), pairl(ls), bre, bim, cre, cim], axis=1))
    shared = {
        "w_in": np.ascontiguousarray(inp["w_in"][0], dtype=f32),
        "w_glu": np.ascontiguousarray(inp["w_glu"][0], dtype=f32),
        "w_pf": np.ascontiguousarray(inp["w_proj_fox"][0], dtype=f32),
        "w_ps": np.ascontiguousarray(inp["w_proj_s5"][0], dtype=f32),
        "w_out": np.ascontiguousarray(inp["w_out"][0], dtype=f32),
        "w_gu": np.ascontiguousarray(inp["w_gate_up"][0], dtype=f32),
        "w_dn": np.ascontiguousarray(inp["w_down"][0], dtype=f32),
        "consts": consts, "oneh": oneh, "s5p": s5p,
    }
    in_maps = []
    x = np.asarray(inp["x"], f32)
    for c in range(2 * n_pairs):
        b, half = c // 2, c % 2
        own = OWN_TILES[half]
        selv = np.array([1.0 if own[j] == 2 * j else 0.0 for j in range(NJ)], f32)
        small = np.concatenate([
            colT(inp["g_mix"][0], DC), colT(inp["g_ffn"][0], DC), np.zeros((128, DC), f32),
            colT(inp["b_gates"][0], 2 * DC), colT(inp["q_norm"][0], 1), colT(inp["k_norm"][0], 1),
            rep(inp["b_fgate"][0]), rep(selv), rep(1.0 - selv), kpos,
            colT(np.asarray(inp["s5_d"][0], f32).reshape(-1), SC), colT(inp["b_glu"][0], SC)], axis=1)
        qpos = np.concatenate([np.arange(t * TT, (t + 1) * TT, dtype=f32) for t in own])
        m = dict(shared)
        m["xs"] = np.ascontiguousarray(x[b])
        m["xo"] = np.ascontiguousarray(np.concatenate([x[b, t * TT:(t + 1) * TT] for t in own], axis=0))
        m["small"] = np.ascontiguousarray(small, dtype=f32)
        m["qpos"] = np.ascontiguousarray(np.broadcast_to(qpos[None, :], (128, NJ * TT)), dtype=f32)
        in_maps.append(m)
    return in_maps


def gather(cfg, results, n_pairs):
    out = np.zeros((n_pairs, SEQ, cfg.D), np.float32)
    for c in range(2 * n_pairs):
        b, half = c // 2, c % 2
        y = results[c]["y"]
        for j, t in enumerate(OWN_TILES[half]):
            out[b, t * TT:(t + 1) * TT] = y[j * TT:(j + 1) * TT]
    return out


def kernel(**inputs):
    cfg = Cfg()
    nc, info = build(cfg)
    in_maps = make_in_maps(cfg, inputs, 4)
    res = run_bass_kernel_spmd(nc, in_maps, core_ids=list(range(8)))
    return gather(cfg, res.results, 4)
```

```python
import math
import numpy as np
import ml_dtypes
import concourse.bass as bass
import concourse.mybir as mybir
from concourse.bass_utils import run_bass_kernel_spmd

F32 = mybir.dt.float32
BF16 = mybir.dt.bfloat16
U8 = mybir.dt.uint8
AF = mybir.ActivationFunctionType
ALU = mybir.AluOpType
AX = mybir.AxisListType
EPS = 1e-6
SEQ = 4096
TT = 512
NT = SEQ // TT
NJ = NT // 2
NKB = SEQ // 128
MAGIC = 12582912.0
TWO_PI = 2.0 * math.pi


class Cfg:
    def __init__(self, D=4096, NH=16, G=64, DFF=11008):
        self.D, self.NH, self.G, self.DFF = D, NH, G, DFF
        self.DC = D // 128
        self.FW = NH * 128
        self.SW = G * 16
        self.SC = self.SW // 128
        self.NGP = G // 2
        self.KF = DFF // 128
        self.COL_K = self.FW
        self.COL_V = 2 * self.FW
        self.COL_F = 3 * self.FW
        self.COL_S5 = 3 * self.FW + NH
        self.COL_GF = self.COL_S5 + self.SW
        self.COL_GS = self.COL_GF + D
        self.INC = self.COL_GS + D


class Sched:
    def __init__(self, nc):
        self.nc = nc
        self.eng = {"pe": nc.tensor, "act": nc.scalar, "dve": nc.vector,
                    "pool": nc.gpsimd, "sp": nc.sync}
        self.sems = {}
        self.count = {}
        for n in ("pe", "act", "dve", "pool"):
            self.sems[n] = nc.alloc_semaphore("s_" + n)
            self.count[n] = 0
        self.rings = {"sp": [], "pool": []}
        for q, n in (("sp", 40), ("pool", 12)):
            for i in range(n):
                nm = f"dq_{q}{i}"
                self.sems[nm] = nc.alloc_semaphore("s_" + nm)
                self.count[nm] = 0
                self.rings[q].append(nm)
        self.dma_idx = {"sp": 0, "pool": 0}
        self.waited = {e: {} for e in self.eng}
        self.last_w = {}
        self.readers = {}
        self.n_wait = 0
        self.n_ops = 0

    def _need(self, eng, reads, writes, is_dma):
        need = {}

        def add(tok, raw):
            if tok is None:
                return
            s, v, ex = tok
            if (not is_dma) and ex == eng:
                if not raw or eng == "pe":
                    return
            if self.waited[eng].get(s, 0) >= v:
                return
            if need.get(s, 0) < v:
                need[s] = v
        for r in reads:
            add(self.last_w.get(r), True)
        for w in writes:
            add(self.last_w.get(w), False)
            for s, (v, ex) in self.readers.get(w, {}).items():
                add((s, v, ex), False)
        return need

    def _emit_waits(self, eng, need):
        e = self.eng[eng]
        for s, v in need.items():
            e.wait_ge(self.sems[s], v)
            self.waited[eng][s] = v
            self.n_wait += 1

    def _record(self, tok, reads, writes):
        s, v, ex = tok
        for w in writes:
            self.last_w[w] = tok
            self.readers[w] = {}
        for r in reads:
            self.readers.setdefault(r, {})[s] = (v, ex)

    def op(self, eng, fn, reads=(), writes=(), inc=True):
        self._emit_waits(eng, self._need(eng, reads, writes, False))
        ins = fn(self.eng[eng])
        self.n_ops += 1
        if inc:
            self.count[eng] += 1
            ins.then_inc(self.sems[eng], 1)
            tok = (eng, self.count[eng], eng)
        else:
            tok = (eng, self.count[eng] + 1, eng)
        self._record(tok, reads, writes)
        return ins

    def dma(self, q, out, in_, reads=(), writes=()):
        ring = self.rings[q]
        sname = ring[self.dma_idx[q] % len(ring)]
        self.dma_idx[q] += 1
        need = self._need(q, reads, writes, True)
        prev = self.count[sname]
        if prev > 0 and self.waited[q].get(sname, 0) < prev:
            need[sname] = max(need.get(sname, 0), prev)
        self._emit_waits(q, need)
        ins = self.eng[q].dma_start(out=out, in_=in_)
        self.count[sname] += 16
        ins.then_inc(self.sems[sname], 16)
        self._record((sname, self.count[sname], "dma"), reads, writes)
        self.n_ops += 1
        return ins

    def barrier(self, engines=None):
        for en in (engines or list(self.eng)):
            e = self.eng[en]
            for s, c in self.count.items():
                if c > 0 and self.waited[en].get(s, 0) < c:
                    e.wait_ge(self.sems[s], c)
                    self.waited[en][s] = c
                    self.n_wait += 1


class Arena:
    def __init__(self, nc, nbytes):
        self.t = nc.alloc_sbuf_tensor("arena", [128, nbytes], U8)
        self.nbytes = nbytes
        self.off = 0
        self.peak = 0

    def mark(self):
        return self.off

    def reset(self, m):
        self.off = m

    def alloc(self, shape, dt):
        esz = 4 if dt == F32 else 2
        n = esz
        for s in shape[1:]:
            n *= s
        self.off = (self.off + 63) // 64 * 64
        assert self.off + n <= self.nbytes, ("arena overflow", self.off, n, self.nbytes)
        ap = self.t[:, self.off:self.off + n].bitcast(dt)
        self.off += n
        self.peak = max(self.peak, self.off)
        if len(shape) == 3:
            ap = ap.rearrange("p (a b) -> p a b", a=shape[1])
        elif len(shape) == 4:
            ap = ap.rearrange("p (a b c) -> p a b c", a=shape[1], b=shape[2])
        return ap


class _Stop(Exception):
    pass


def build(cfg, stop=None):
    try:
        return _build(cfg, stop)
    except _Stop as ex:
        nc, S, A = ex.args
        S.barrier()
        return nc, dict(ops=S.n_ops, waits=S.n_wait, peak=A.peak, stopped=stop)


def _build(cfg, stop):
    D, NH, G, DFF = cfg.D, cfg.NH, cfg.G, cfg.DFF
    DC, FW, SW, SC, NGP, KF = cfg.DC, cfg.FW, cfg.SW, cfg.SC, cfg.NGP, cfg.KF
    nc = bass.Bass("TRN2", target_bir_lowering=False)

    def din(name, shape, dt=F32):
        return nc.dram_tensor(name, list(shape), dt, kind="ExternalInput").ap()

    xs = din("xs", [SEQ, D])
    xo = din("xo", [NJ * TT, D])
    w_in = din("w_in", [D, cfg.INC])
    w_glu = din("w_glu", [SW, SW])
    w_pf = din("w_pf", [FW, D])
    w_ps = din("w_ps", [SW, D])
    w_out = din("w_out", [D, D])
    w_gu = din("w_gu", [D, 2 * DFF])
    w_dn = din("w_dn", [DFF, D])
    NSM = 3 * DC + 2 * DC + 2 + NH + 8 + NKB + 2 * SC
    small_d = din("small", [128, NSM])
    qpos_d = din("qpos", [128, NJ * TT])
    consts_d = din("consts", [128, 4 * 128])
    oneh_d = din("oneh", [128, NH * 128], BF16)
    s5p_d = din("s5p", [128, 3 * NGP + 4 * NGP * 16])
    y_d = nc.dram_tensor("y", [NJ * TT, D], F32, kind="ExternalOutput").ap()
    kT_d = nc.dram_tensor("kT_s", [NH, 128, SEQ], BF16, kind="Internal").ap()
    vS_d = nc.dram_tensor("vS_s", [NH, 128, NKB, 128], BF16, kind="Internal").ap()
    yS_d = nc.dram_tensor("yS_s", [NJ, 128, SC, TT], BF16, kind="Internal").ap()
    hS_d = nc.dram_tensor("hS_s", [NJ * TT, D], F32, kind="Internal").ap()
    wtot = D * cfg.INC + SW * SW + FW * D + SW * D + D * D + D * 2 * DFF + DFF * D
    NSLOT = wtot // (128 * 8 * 512) + 64
    SPT = 192
    wc_list = [nc.dram_tensor(f"wc_s{i}", [SPT, 128, 8 * 512], BF16, kind="Internal").ap()
               for i in range((NSLOT + SPT - 1) // SPT)]

    class _WC:
        def __getitem__(self, idx):
            slot = idx[0]
            return wc_list[slot // SPT][(slot % SPT,) + tuple(idx[1:])]
    wc_d = _WC()

    S = Sched(nc)
    A = Arena(nc, 206 * 1024)

    def chk(name):
        if stop == name:
            raise _Stop(nc, S, A)
    ps = [nc.alloc_psum_tensor(f"ps{i}", [128, 512], F32) for i in range(8)]

    small = A.alloc([128, NSM], F32)
    consts = A.alloc([128, 512], F32)
    oneh = A.alloc([128, NH, 128], BF16)
    ident_f = consts[:, 0:128]
    tri_f = consts[:, 128:256]
    iota1 = consts[:, 256:384]
    ident_b = A.alloc([128, 128], BF16)
    ones_b = A.alloc([128, 128], BF16)
    ones_f = A.alloc([128, 128], F32)
    qn_s = A.alloc([128, 1], F32)
    Fall = A.alloc([128, NKB, NH], F32)
    carry = A.alloc([128, NH], F32)
    Fref = A.alloc([128, NJ, NH], F32)
    stat = A.alloc([128, 64], F32)
    eps_col = stat[:, 32:33]
    o = 0
    gmixT = small[:, o:o + DC]; o += DC
    gffnT = small[:, o:o + DC]; o += DC
    o += DC
    bgT = small[:, o:o + 2 * DC]; o += 2 * DC
    qn_c = small[:, o:o + 1]; o += 1
    kn_c = small[:, o:o + 1]; o += 1
    bf_rep = small[:, o:o + NH]; o += NH
    sel = small[:, o:o + 4]; o += 4
    nsel = small[:, o:o + 4]; o += 4
    kpos = small[:, o:o + NKB]; o += NKB
    d_l = small[:, o:o + SC]; o += SC
    bglu_l = small[:, o:o + SC]; o += SC
    assert o == NSM
    S.dma("sp", small, small_d, writes=["small"])
    S.dma("sp", consts, consts_d, writes=["consts"])
    S.dma("sp", oneh.rearrange("p a b -> p (a b)"), oneh_d, writes=["oneh"])
    S.op("dve", lambda e: e.tensor_copy(ident_b, ident_f), reads=["consts"], writes=["ident_b"])
    S.op("dve", lambda e: e.memset(ones_b, 1.0), writes=["ones_b"])
    S.op("dve", lambda e: e.memset(ones_f, 1.0), writes=["ones_f"])
    S.op("dve", lambda e: e.memset(carry, 0.0), writes=["carry"])
    S.op("dve", lambda e: e.memset(eps_col, EPS), writes=["eps_col"])
    S.op("dve", lambda e: e.tensor_scalar(qn_s, qn_c, 1.0 / math.sqrt(128.0), None, ALU.mult),
         reads=["small"], writes=["qn_s"])

    uT = A.alloc([128, DC, TT], BF16)
    NWT = 3
    KG = 8
    wt = [A.alloc([128, KG, 512], BF16) for _ in range(NWT)]
    wt_i = [0]
    m_base = A.mark()

    wc_slots = {}
    nfirst = [0]

    def load_w(w_ap, k0, kg, c0, ncl):
        b = wt_i[0]
        wt_i[0] = (b + 1) % NWT
        key = (w_ap.name, k0, c0, ncl)
        dst = wt[b][:, 0:kg, 0:ncl]
        if key in wc_slots:
            slot = wc_slots[key]
            S.dma("pool", dst, wc_d[slot, :, 0:kg * ncl].rearrange("p (k n) -> p k n", k=kg),
                  reads=[("wc", slot)], writes=[("wt", b)])
        else:
            src = w_ap[k0 * 128:(k0 + kg) * 128, c0:c0 + ncl].rearrange("(kc p) n -> p kc n", p=128)
            S.dma("pool", dst, src, writes=[("wt", b)])
            nfirst[0] += 1
            seqw = (w_ap.name == "w_in" and cfg.COL_K <= c0 < cfg.COL_GF)
            if seqw or nfirst[0] % 2 == 0:
                slot = len(wc_slots)
                assert slot < NSLOT
                wc_slots[key] = slot
                S.dma("sp", wc_d[slot, :, 0:kg * ncl].rearrange("p (k n) -> p k n", k=kg), dst,
                      reads=[("wt", b)], writes=[("wc", slot)])
        return b

    def gemm_fm(actT, ares, KC, w_ap, c0, ncols, epi, gcols=512, bank_sets=((0, 1, 2, 3), (4, 5, 6, 7)),
                gi0=0):
        gi = gi0
        for cg0 in range(c0, c0 + ncols, gcols):
            ncg = min(gcols, c0 + ncols - cg0)
            nct = ncg // 128
            banks = bank_sets[gi % len(bank_sets)]
            gi += 1
            for k0 in range(0, KC, KG):
                kg = min(KG, KC - k0)
                b = load_w(w_ap, k0, kg, cg0, ncg)
                for ct in range(nct):
                    for kc in range(kg):
                        S.op("pe", lambda e: e.matmul(ps[banks[ct]][:, :], wt[b][:, kc, ct * 128:(ct + 1) * 128],
                                                      actT[:, k0 + kc, :], start=(k0 + kc == 0),
                                                      stop=(k0 + kc == KC - 1)),
                             reads=[("wt", b), (ares, k0 + kc)], writes=[("ps", banks[ct])], inc=(kc == kg - 1))
                if tick[0] is not None and tick_on[0]:
                    tick[0]()
            for ct in range(nct):
                epi((cg0 - c0) // 128 + ct, banks[ct])
        return gi

    def gemm_tm(actT, ares, KC, w_ap, c0, ncols, epi, bank_sets=((0, 1, 2, 3), (4, 5, 6, 7))):
        gi = 0
        for cb0 in range(c0, c0 + ncols, 512):
            ncb = min(512, c0 + ncols - cb0)
            banks = bank_sets[gi % len(bank_sets)]
            for k0 in range(0, KC, KG):
                kg = min(KG, KC - k0)
                b = load_w(w_ap, k0, kg, cb0, ncb)
                for tb in range(4):
                    for kc in range(kg):
                        S.op("pe", lambda e: e.matmul(ps[banks[tb]][:, 0:ncb], actT[:, k0 + kc, tb * 128:(tb + 1) * 128],
                                                      wt[b][:, kc, 0:ncb], start=(k0 + kc == 0),
                                                      stop=(k0 + kc == KC - 1)),
                             reads=[("wt", b), (ares, k0 + kc)], writes=[("ps", banks[tb])], inc=(kc == kg - 1))
                if tick[0] is not None and tick_on[0]:
                    tick[0]()
            for tb in range(4):
                epi(gi, tb, banks[tb], ncb)
            gi += 1

    import os as _os
    DBG = _os.environ.get("KDBG", "")
    tr_i = [0]
    TRB = [6, 7]
    NRB = [6, 7]
    TR_ALL_ACT = [False]
    tick = [None]
    tick_on = [True]

    def transpose_to_uT(src, sres, tb, gT, gres):
        for c8 in range(0, DC, 8):
            n8 = min(8, DC - c8)
            bank = TRB[tr_i[0] % 2]
            tr_i[0] += 1
            pb = ps[bank][:, :].bitcast(BF16)
            for i in range(n8):
                S.op("pe", lambda e: e.transpose(pb[:, i * 128:(i + 1) * 128], src[:, (c8 + i) * 128:(c8 + i + 1) * 128],
                                                 ident_b),
                     reads=[sres, "ident_b"], writes=[("ps", bank)], inc=(i == n8 - 1))
            for i in range(n8):
                c = c8 + i
                dst = uT[:, c, tb * 128:(tb + 1) * 128]
                if "e" in DBG:
                    continue
                if "g" in DBG:
                    S.op("dve", lambda e: e.tensor_copy(dst, pb[:, i * 128:(i + 1) * 128]),
                         reads=[("ps", bank), gres], writes=[("uT", c)])
                    continue
                if bank == TRB[0] or TR_ALL_ACT[0]:
                    S.op("act", lambda e: e.activation(dst, pb[:, i * 128:(i + 1) * 128], AF.Copy, scale=gT[:, c:c + 1]),
                         reads=[("ps", bank), gres], writes=[("uT", c)])
                else:
                    S.op("dve", lambda e: e.tensor_scalar(dst, pb[:, i * 128:(i + 1) * 128], gT[:, c:c + 1], None, ALU.mult),
                         reads=[("ps", bank), gres], writes=[("uT", c)])

    def rstd_from(col, scale):
        S.op("dve", lambda e: e.tensor_scalar(col, col, scale, EPS, ALU.mult, ALU.add), reads=["rs"], writes=["rs"])
        S.op("act", lambda e: e.activation(col, col, AF.Sqrt), reads=["rs"], writes=["rs"])
        S.op("dve", lambda e: e.reciprocal(col, col), reads=["rs"], writes=["rs"])

    def norm1(x_d, r0, xst, xn):
        ssq = stat[:, 0:1]
        for tb in range(4):
            S.dma("sp", xst, x_d[r0 + tb * 128:r0 + (tb + 1) * 128, :], writes=["xst"])
            S.op("dve", lambda e: e.memset(ssq, 0.0), reads=["rs"], writes=["rs"])
            if "a" not in DBG:
                S.op("act", lambda e: e.activation(xn, xst, AF.Square, accum_out=ssq), reads=["xst", "rs"],
                     writes=["xn", "rs"])
            if "b" not in DBG:
                rstd_from(ssq, 1.0 / D)
            if "c" not in DBG:
                S.op("act", lambda e: e.activation(xn, xst, AF.Copy, scale=ssq), reads=["xst", "rs"], writes=["xn"])
            if "d" not in DBG:
                transpose_to_uT(xn, "xn", tb, gmixT, "small")

    nrm_i = [0]

    def qknorm(bank, gain_col, gres, out_ap, out_res, tmp):
        sqs, rstds = tmp
        i_ = nrm_i[0] % 2
        nb = NRB[nrm_i[0] % len(NRB)]
        nrm_i[0] += 1
        sq, rstd = sqs[i_], rstds[i_]
        S.op("act", lambda e: e.activation(sq, ps[bank][:, :], AF.Square), reads=[("ps", bank)], writes=[("sq", i_)])
        S.op("pe", lambda e: e.matmul(ps[nb][:, :], ones_b, sq, start=True, stop=True),
             reads=["ones_b", ("sq", i_)], writes=[("ps", nb)])
        S.op("act", lambda e: e.activation(rstd, ps[nb][:, :], AF.Ln, bias=eps_col, scale=1.0 / 128.0),
             reads=[("ps", nb), "eps_col"], writes=[("rstd", i_)])
        S.op("act", lambda e: e.activation(rstd, rstd, AF.Exp, scale=-0.5), reads=[("rstd", i_)], writes=[("rstd", i_)])
        S.op("dve", lambda e: e.scalar_tensor_tensor(out_ap, ps[bank][:, :], gain_col, rstd, ALU.mult, ALU.mult),
             reads=[("ps", bank), ("rstd", i_), gres], writes=[out_res])

    def range_reduce(t):
        tmp = rr_tmp[:, 0:t.shape[-1]] if len(t.shape) == 2 else None
        S.op("dve", lambda e: e.tensor_scalar(tmp, t, 1.0 / TWO_PI, MAGIC, ALU.mult, ALU.add), reads=["s5c"], writes=["s5c"])
        S.op("dve", lambda e: e.tensor_scalar(tmp, tmp, MAGIC, -TWO_PI, ALU.subtract, ALU.mult), reads=["s5c"], writes=["s5c"])
        S.op("dve", lambda e: e.tensor_tensor(t, t, tmp, ALU.add), reads=["s5c"], writes=["s5c"])

    xst = A.alloc([128, D], F32)
    xn = A.alloc([128, D], BF16)
    Blhs = A.alloc([128, NGP, 2, 128], BF16)
    Clhs = A.alloc([128, NGP, 2, 128], BF16)
    ctab = A.alloc([128, NGP, 128], BF16)
    stab = A.alloc([128, NGP, 128], BF16)
    sc = A.alloc([128, 16, NGP], F32)
    hst = A.alloc([128, NGP, 2], F32)
    m_setup = A.mark()
    s5p = A.alloc([128, 3 * NGP + 4 * NGP * 16], F32)
    rr_tmp = A.alloc([128, 128], F32)
    ang_t = A.alloc([128, 128], F32)
    zt = A.alloc([128, 128], F32)
    bb = A.alloc([128, 2, NGP, 16], F32)

    S.dma("sp", s5p, s5p_d, writes=["s5p"])
    lr = s5p[:, 0:NGP]
    li = s5p[:, NGP:2 * NGP]
    lsx = s5p[:, 2 * NGP:3 * NGP]
    o = 3 * NGP
    bre = s5p[:, o:o + NGP * 16].rearrange("p (a b) -> p a b", a=NGP); o += NGP * 16
    bim = s5p[:, o:o + NGP * 16].rearrange("p (a b) -> p a b", a=NGP); o += NGP * 16
    cre = s5p[:, o:o + NGP * 16].rearrange("p (a b) -> p a b", a=NGP); o += NGP * 16
    cim = s5p[:, o:o + NGP * 16].rearrange("p (a b) -> p a b", a=NGP); o += NGP * 16
    dt_, mag, ang, lbr, lbi, den, nre, facr, faci, c128, s128, tA, tB, tC = [sc[:, i, :] for i in range(14)]

    def s5op(eng, fn):
        S.op(eng, fn, reads=["s5p", "s5c"], writes=["s5c"])
    s5op("act", lambda e: e.activation(dt_, lsx, AF.Exp))
    s5op("dve", lambda e: e.tensor_tensor(tA, lr, dt_, ALU.mult))
    s5op("act", lambda e: e.activation(mag, tA, AF.Exp))
    s5op("dve", lambda e: e.tensor_tensor(ang, li, dt_, ALU.mult))
    s5op("dve", lambda e: e.tensor_copy(tA, ang))
    range_reduce(tA)
    s5op("act", lambda e: e.activation(tB, tA, AF.Sin))
    s5op("dve", lambda e: e.tensor_scalar(tA, ang, math.pi / 2, None, ALU.add))
    range_reduce(tA)
    s5op("act", lambda e: e.activation(tC, tA, AF.Sin))
    s5op("dve", lambda e: e.tensor_tensor(lbr, mag, tC, ALU.mult))
    s5op("dve", lambda e: e.tensor_tensor(lbi, mag, tB, ALU.mult))
    s5op("dve", lambda e: e.tensor_scalar(tA, ang, 128.0, None, ALU.mult))
    range_reduce(tA)
    s5op("act", lambda e: e.activation(s128, tA, AF.Sin))
    s5op("dve", lambda e: e.tensor_scalar(tA, ang, 128.0, math.pi / 2, ALU.mult, ALU.add))
    range_reduce(tA)
    s5op("act", lambda e: e.activation(c128, tA, AF.Sin))
    s5op("dve", lambda e: e.tensor_tensor(den, lr, lr, ALU.mult))
    s5op("dve", lambda e: e.tensor_tensor(tA, li, li, ALU.mult))
    s5op("dve", lambda e: e.tensor_tensor(den, den, tA, ALU.add))
    s5op("dve", lambda e: e.reciprocal(den, den))
    s5op("dve", lambda e: e.tensor_scalar(nre, lbr, -1.0, None, ALU.add))
    s5op("dve", lambda e: e.tensor_tensor(tA, nre, lr, ALU.mult))
    s5op("dve", lambda e: e.tensor_tensor(tB, lbi, li, ALU.mult))
    s5op("dve", lambda e: e.tensor_tensor(tA, tA, tB, ALU.add))
    s5op("dve", lambda e: e.tensor_tensor(facr, tA, den, ALU.mult))
    s5op("dve", lambda e: e.tensor_tensor(tA, lbi, lr, ALU.mult))
    s5op("dve", lambda e: e.tensor_tensor(tB, nre, li, ALU.mult))
    s5op("dve", lambda e: e.tensor_tensor(tA, tA, tB, ALU.subtract))
    s5op("dve", lambda e: e.tensor_tensor(faci, tA, den, ALU.mult))
    s5op("dve", lambda e: e.memset(hst, 0.0))
    s5op("dve", lambda e: e.memset(Blhs, 0.0))
    s5op("dve", lambda e: e.memset(Clhs, 0.0))
    for gp in range(NGP):
        q4 = gp % 4
        s5op("dve", lambda e: e.tensor_scalar(bb[:, 0, gp, :], bim[:, gp, :], faci[:, gp:gp + 1], None, ALU.mult))
        s5op("dve", lambda e: e.scalar_tensor_tensor(bb[:, 0, gp, :], bre[:, gp, :], facr[:, gp:gp + 1], bb[:, 0, gp, :],
                                                     ALU.mult, ALU.subtract))
        s5op("dve", lambda e: e.tensor_scalar(bb[:, 1, gp, :], bre[:, gp, :], faci[:, gp:gp + 1], None, ALU.mult))
        s5op("dve", lambda e: e.scalar_tensor_tensor(bb[:, 1, gp, :], bim[:, gp, :], facr[:, gp:gp + 1], bb[:, 1, gp, :],
                                                     ALU.mult, ALU.add))
        for ri in range(2):
            s5op("dve", lambda e: e.memset(zt, 0.0))
            s5op("dve", lambda e: e.tensor_copy(zt[0:64, q4 * 32:q4 * 32 + 16], bb[0:64, ri, gp, :]))
            s5op("dve", lambda e: e.tensor_copy(zt[64:128, q4 * 32 + 16:q4 * 32 + 32], bb[64:128, ri, gp, :]))
            S.op("pe", lambda e: e.transpose(ps[0][:, 0:128], zt, ident_f), reads=["s5c", "consts"], writes=[("ps", 0)])
            S.op("act", lambda e: e.activation(Blhs[:, gp, ri, :], ps[0][:, 0:128], AF.Copy), reads=[("ps", 0)],
                 writes=["s5c"])
        s5op("dve", lambda e: e.tensor_copy(Clhs[0:64, gp, 0, q4 * 32:q4 * 32 + 16], cre[0:64, gp, :]))
        s5op("dve", lambda e: e.tensor_copy(Clhs[64:128, gp, 0, q4 * 32 + 16:q4 * 32 + 32], cre[64:128, gp, :]))
        s5op("dve", lambda e: e.tensor_scalar(Clhs[0:64, gp, 1, q4 * 32:q4 * 32 + 16], cim[0:64, gp, :], -1.0, None, ALU.mult))
        s5op("dve", lambda e: e.tensor_scalar(Clhs[64:128, gp, 1, q4 * 32 + 16:q4 * 32 + 32], cim[64:128, gp, :], -1.0, None,
                                              ALU.mult))
        s5op("dve", lambda e: e.tensor_scalar(ang_t, iota1, ang[:, gp:gp + 1], None, ALU.mult))
        range_reduce(ang_t)
        s5op("act", lambda e: e.activation(stab[:, gp, :], ang_t, AF.Sin))
        s5op("dve", lambda e: e.tensor_scalar(ang_t, iota1, ang[:, gp:gp + 1], math.pi / 2, ALU.mult, ALU.add))
        range_reduce(ang_t)
        s5op("act", lambda e: e.activation(ctab[:, gp, :], ang_t, AF.Sin))

    S.barrier()
    A.reset(m_setup)
    s5u2 = [A.alloc([128, SC, TT], BF16)]

    class _TS:
        pass
    tsets = []
    for _k in range(2):
        T_ = _TS()
        for nm in ("bur", "bui", "t1", "t2", "t3", "t4", "wre", "wim"):
            setattr(T_, nm, A.alloc([128, TT], F32))
        T_.cini = A.alloc([128, 8], F32)
        tsets.append(T_)
    xr_b2 = [A.alloc([128, TT], BF16) for _ in range(2)]
    xi_b2 = [A.alloc([128, TT], BF16) for _ in range(2)]
    ysel = A.alloc([128, SC, TT], BF16)
    yt = A.alloc([128, TT], F32)
    sqb = [A.alloc([128, TT], BF16) for _ in range(2)]
    rstd = [A.alloc([128, TT], F32) for _ in range(2)]
    kout = [A.alloc([128, TT], BF16) for _ in range(2)]
    vstg = [A.alloc([128, 512], BF16) for _ in range(2)]
    ftmp = A.alloc([128, 4, NH], F32)

    chk('setup')

    def bc4(tab_gp):
        return tab_gp.unsqueeze(1).broadcast_to([128, 4, 128])

    def v4(ap):
        return ap.rearrange("p (a b) -> p a b", a=4)

    def s5_front_gen(t, gp, j):
        k = gp % 2
        T = tsets[k]
        c = gp // 4
        bk = (0, 1) if k == 0 else (2, 3)
        s5u = s5u2[0]
        sres = ("s5u", 0, c)
        xr_b, xi_b = xr_b2[k], xi_b2[k]

        def R(n):
            return (n, k)
        S.op("pe", lambda e: e.matmul(ps[bk[0]][:, :], Blhs[:, gp, 0, :], s5u[:, c, :], start=True, stop=True),
             reads=["s5c", sres], writes=[("ps", bk[0])])
        S.op("pe", lambda e: e.matmul(ps[bk[1]][:, :], Blhs[:, gp, 1, :], s5u[:, c, :], start=True, stop=True),
             reads=["s5c", sres], writes=[("ps", bk[1])])
        S.op("act", lambda e: e.activation(T.bur, ps[bk[0]][:, :], AF.Copy), reads=[("ps", bk[0])], writes=[R("bur")])
        S.op("act", lambda e: e.activation(T.bui, ps[bk[1]][:, :], AF.Copy), reads=[("ps", bk[1])], writes=[R("bui")])
        yield
        ct4, st4 = bc4(ctab[:, gp, :]), bc4(stab[:, gp, :])
        S.op("dve", lambda e: e.tensor_tensor(v4(T.t1), v4(T.bur), ct4, ALU.mult), reads=[R("bur"), "s5c"], writes=[R("t1")])
        S.op("pool", lambda e: e.tensor_tensor(v4(T.t3), v4(T.bui), ct4, ALU.mult), reads=[R("bui"), "s5c"], writes=[R("t3")])
        yield
        S.op("dve", lambda e: e.tensor_tensor(v4(T.t2), v4(T.bui), st4, ALU.mult), reads=[R("bui"), "s5c"], writes=[R("t2")])
        S.op("pool", lambda e: e.tensor_tensor(v4(T.t4), v4(T.bur), st4, ALU.mult), reads=[R("bur"), "s5c"], writes=[R("t4")])
        yield
        S.op("dve", lambda e: e.tensor_tensor(T.t1, T.t1, T.t2, ALU.add), reads=[R("t1"), R("t2")], writes=[R("t1")])
        S.op("pool", lambda e: e.tensor_tensor(T.t3, T.t3, T.t4, ALU.subtract), reads=[R("t3"), R("t4")], writes=[R("t3")])
        yield "mid"
        rb = mag[:, gp:gp + 1].broadcast_to([128, 128])
        cini = T.cini
        for sb in range(4):
            ir = hst[:, gp, 0:1] if sb == 0 else cini[:, 0:1]
            ii = hst[:, gp, 1:2] if sb == 0 else cini[:, 1:2]
            sl = slice(sb * 128, (sb + 1) * 128)
            S.op("dve", lambda e: e.tensor_tensor_scan(T.wre[:, sl], rb, T.t1[:, sl], ir, ALU.mult, ALU.add),
                 reads=[R("t1"), "s5c", R("cini"), ("hst", gp)], writes=[R("wre")])
            S.op("dve", lambda e: e.tensor_tensor_scan(T.wim[:, sl], rb, T.t3[:, sl], ii, ALU.mult, ALU.add),
                 reads=[R("t3"), "s5c", R("cini"), ("hst", gp)], writes=[R("wim")])
            yield
            lr_, li_ = T.wre[:, sb * 128 + 127:sb * 128 + 128], T.wim[:, sb * 128 + 127:sb * 128 + 128]
            dr = hst[:, gp, 0:1] if sb == 3 else cini[:, 0:1]
            di = hst[:, gp, 1:2] if sb == 3 else cini[:, 1:2]
            dres = ("hst", gp) if sb == 3 else R("cini")
            S.op("dve", lambda e: e.tensor_scalar(cini[:, 2:3], li_, s128[:, gp:gp + 1], None, ALU.mult),
                 reads=[R("wim"), "s5c"], writes=[R("cini2")])
            S.op("dve", lambda e: e.tensor_scalar(cini[:, 3:4], lr_, s128[:, gp:gp + 1], None, ALU.mult),
                 reads=[R("wre"), "s5c"], writes=[R("cini2")])
            yield
            S.op("dve", lambda e: e.scalar_tensor_tensor(dr, lr_, c128[:, gp:gp + 1], cini[:, 2:3], ALU.mult, ALU.subtract),
                 reads=[R("wre"), R("cini2"), "s5c"], writes=[dres])
            S.op("dve", lambda e: e.scalar_tensor_tensor(di, li_, c128[:, gp:gp + 1], cini[:, 3:4], ALU.mult, ALU.add),
                 reads=[R("wim"), R("cini2"), "s5c"], writes=[dres])
            yield
        S.op("pool", lambda e: e.tensor_tensor(v4(T.bur), v4(T.wre), ct4, ALU.mult), reads=[R("wre"), "s5c"], writes=[R("bur")])
        S.op("dve", lambda e: e.tensor_tensor(v4(T.t2), v4(T.wim), st4, ALU.mult), reads=[R("wim"), "s5c"], writes=[R("t2")])
        yield
        S.op("pool", lambda e: e.tensor_tensor(v4(T.bui), v4(T.wim), ct4, ALU.mult), reads=[R("wim"), "s5c"], writes=[R("bui")])
        S.op("dve", lambda e: e.tensor_tensor(v4(T.t4), v4(T.wre), st4, ALU.mult), reads=[R("wre"), "s5c"], writes=[R("t4")])
        yield
        S.op("pool", lambda e: e.tensor_tensor(xr_b, T.bur, T.t2, ALU.subtract), reads=[R("bur"), R("t2")], writes=[("xr_b", k)])
        S.op("pool", lambda e: e.tensor_tensor(xi_b, T.bui, T.t4, ALU.add), reads=[R("bui"), R("t4")], writes=[("xi_b", k)])
        yield

    def s5_back(t, gp, j):
        c = gp // 4
        yb = 4 + (c % 2)
        s5u = s5u2[0]
        sres = ("s5u", 0, c)
        xr_b, xi_b = xr_b2[gp % 2], xi_b2[gp % 2]
        xrr, xir = ("xr_b", gp % 2), ("xi_b", gp % 2)
        q4 = gp % 4
        S.op("pe", lambda e: e.matmul(ps[yb][:, :], Clhs[:, gp, 0, :], xr_b, start=(q4 == 0), stop=False),
             reads=["s5c", xrr], writes=[("ps", yb)], inc=False)
        S.op("pe", lambda e: e.matmul(ps[yb][:, :], Clhs[:, gp, 1, :], xi_b, start=False, stop=(q4 == 3 or gp == NGP - 1)),
             reads=["s5c", xir], writes=[("ps", yb)])
        if q4 == 3 or gp == NGP - 1:
            S.op("dve", lambda e: e.scalar_tensor_tensor(yt, s5u[:, c, :], d_l[:, c:c + 1], ps[yb][:, :], ALU.mult, ALU.add),
                 reads=[sres, "small", ("ps", yb)], writes=["yt"])
            if t % 2 == 0:
                S.op("dve", lambda e: e.tensor_scalar(ysel[:, c, :], yt, sel[:, j:j + 1], None, ALU.mult),
                     reads=["yt", "small"], writes=[("ysel", c)])
            else:
                S.op("dve", lambda e: e.scalar_tensor_tensor(ysel[:, c, :], yt, nsel[:, j:j + 1], ysel[:, c, :], ALU.mult, ALU.add),
                     reads=["yt", "small", ("ysel", c)], writes=[("ysel", c)])

    def s5_gen(t):
        j = t // 2
        pending = []
        for g0 in range(0, NGP, 2):
            gens = [s5_front_gen(t, g0, j), s5_front_gen(t, g0 + 1, j)]
            alive = list(gens)
            n_mid = 0
            while alive:
                for g in list(alive):
                    try:
                        r = next(g)
                    except StopIteration:
                        alive.remove(g)
                        continue
                    if r == "mid":
                        n_mid += 1
                        if n_mid == 2:
                            for gp_ in pending:
                                s5_back(t, gp_, j)
                            pending = []
            pending = [g0, g0 + 1]
            yield
        for gp_ in pending:
            s5_back(t, gp_, j)
        if t % 2 == 1:
            S.dma("sp", yS_d[j], ysel, reads=[("ysel", c) for c in range(SC)], writes=[("yS", j)])

    def make_tick(gen):
        def _t():
            try:
                next(gen)
            except StopIteration:
                tick[0] = None
        return _t

    def drain(gen):
        for _ in gen:
            pass

    pend = [None]
    for t in range(NT):
        j = t // 2
        norm1(xs, t * TT, xst, xn)
        chk('seq_norm')
        ko_i = [0]

        def epi_k(ct, bank):
            kb_ = ko_i[0] % 2
            ko_i[0] += 1
            qknorm(bank, kn_c, "small", kout[kb_], ("kout", kb_), (sqb, rstd))
            S.dma("sp", kT_d[ct, :, t * TT:(t + 1) * TT], kout[kb_], reads=[("kout", kb_)], writes=[("kT", ct, t)])
        tick_on[0] = 'k' not in DBG
        gemm_fm(uT, "uT", DC, w_in, cfg.COL_K, FW, epi_k, gcols=256, bank_sets=((0, 1), (2, 3), (4, 5)))
        chk('seq_k')
        vs_i = [0]

        def epi_v(cb, tb, bank, ncb):
            kb = t * 4 + tb
            vb_ = vs_i[0] % 2
            vs_i[0] += 1
            nh_ = ncb // 128
            S.op("act", lambda e: e.activation(vstg[vb_][:, 0:ncb], ps[bank][:, 0:ncb], AF.Copy),
                 reads=[("ps", bank)], writes=[("vstg", vb_)])
            S.dma("sp", vS_d[cb * 4:cb * 4 + nh_, :, kb, :].rearrange("h p d -> p h d"),
                  vstg[vb_][:, 0:ncb].rearrange("p (h d) -> p h d", d=128),
                  reads=[("vstg", vb_)], writes=[("vS", cb, kb)])
        tick_on[0] = 'v' not in DBG
        gemm_tm(uT, "uT", DC, w_in, cfg.COL_V, FW, epi_v)

        chk('seq_v')
        def epi_f(cb, tb, bank, ncb):
            kb = t * 4 + tb
            fl = ftmp[:, 0, :]
            fe = ftmp[:, 1, :]
            lf = ftmp[:, 2, :]
            if t % 2 == 0 and tb == 0:
                S.op("dve", lambda e: e.tensor_copy(Fref[:, j, :], carry), reads=["carry"], writes=["Fref"])
            S.op("dve", lambda e: e.tensor_tensor(fl, ps[bank][:, 0:NH], bf_rep, ALU.add),
                 reads=[("ps", bank), "small"], writes=["fl"])
            S.op("act", lambda e: e.activation(fe, fl, AF.Exp, scale=-1.0), reads=["fl"], writes=["fe"])
            S.op("act", lambda e: e.activation(lf, fe, AF.Ln, bias=1.0), reads=["fe"], writes=["lf"])
            S.op("pe", lambda e: e.matmul(ps[bank][:, 128:128 + NH], tri_f, lf, start=True, stop=True),
                 reads=["consts", "lf"], writes=[("ps", bank)], inc=False)
            S.op("pe", lambda e: e.matmul(ps[bank][:, 256:256 + NH], ones_f, lf, start=True, stop=True),
                 reads=["ones_f", "lf"], writes=[("ps", bank)])
            S.op("dve", lambda e: e.tensor_tensor(Fall[:, kb, :], carry, ps[bank][:, 128:128 + NH], ALU.subtract),
                 reads=["carry", ("ps", bank)], writes=["Fall"])
            S.op("dve", lambda e: e.tensor_tensor(carry, carry, ps[bank][:, 256:256 + NH], ALU.subtract),
                 reads=["carry", ("ps", bank)], writes=["carry"])
        tick_on[0] = 'f' not in DBG
        gemm_tm(uT, "uT", DC, w_in, cfg.COL_F, NH, epi_f)
        chk('seq_f')

        def epi_s5(ct, bank):
            S.op("act", lambda e: e.activation(s5u2[0][:, ct, :], ps[bank][:, :], AF.Copy), reads=[("ps", bank)],
                 writes=[("s5u", 0, ct)])
        tick_on[0] = False
        gemm_fm(uT, "uT", DC, w_in, cfg.COL_S5, SW, epi_s5, bank_sets=((4, 5, 6, 7),))
        tick_on[0] = True
        chk('seq_s5in')
        if pend[0] is not None:
            drain(pend[0])
        pend[0] = s5_gen(t)
        tick[0] = make_tick(pend[0])
        tick[0] = None
        drain(pend[0])
        pend[0] = None
        chk('seq_s5')
    tick[0] = None
    S.barrier()
    seq_peak = A.peak
    chk('seq')

    TRB[:] = [6, 7]
    NRB[:] = [6, 7]
    TR_ALL_ACT[0] = False
    A.reset(m_base)
    attnT = A.alloc([128, NH, TT], BF16)
    ssmT = A.alloc([128, SC, TT], BF16)
    mergedT = A.alloc([128, DC, TT], BF16)
    m_own = A.mark()

    for j in range(NJ):
        nkb = (2 * j + 2) * 4
        A.reset(m_own)
        QT = A.alloc([128, NH, TT], BF16)
        m_a = A.mark()
        xst = A.alloc([128, D], F32)
        xn = A.alloc([128, D], BF16)
        sqb = [A.alloc([128, TT], BF16) for _ in range(2)]
        rstd = [A.alloc([128, TT], F32) for _ in range(2)]
        norm1(xo, j * TT, xst, xn)

        def epi_q(ct, bank):
            qknorm(bank, qn_s, "qn_s", QT[:, ct, :], ("QT", ct), (sqb, rstd))
        gemm_fm(uT, "uT", DC, w_in, 0, FW, epi_q, gcols=256, bank_sets=((0, 1), (2, 3), (4, 5)))
        S.barrier()
        chk('own_q')
        A.reset(m_a)
        kst = [A.alloc([128, SEQ], BF16) for _ in range(2)]
        vst = [A.alloc([128, NKB, 128], BF16) for _ in range(2)]
        maskadd = A.alloc([128, 8, TT], BF16)
        qposb = A.alloc([128, TT], F32)
        mk = A.alloc([128, TT], F32)
        biasK = A.alloc([128, NKB, NH], F32)
        fqt = A.alloc([128, 4, 48], F32)
        fqh = A.alloc([128, TT], BF16)
        fqhf = A.alloc([128, TT], F32)
        fql = A.alloc([128, TT], BF16)
        fqhl = A.alloc([128, TT], BF16)
        PT = [A.alloc([128, TT], BF16) for _ in range(3)]
        mt = [A.alloc([128, TT], F32) for _ in range(2)]
        rc = A.alloc([128, TT], F32)
        S.dma("sp", qposb, qpos_d[:, j * TT:(j + 1) * TT], writes=["qposb"])
        S.op("dve", lambda e: e.memset(fqt, 0.0), writes=["fqt"])
        S.op("dve", lambda e: e.memset(fqhl, 0.0), writes=["fqhl"])
        for tb in range(4):
            fa = Fall[:, (2 * j) * 4 + tb, :]
            fb = Fall[:, (2 * j + 1) * 4 + tb, :]
            S.op("dve", lambda e: e.tensor_scalar(fqt[:, tb, 0:NH], fa, sel[:, j:j + 1], None, ALU.mult),
                 reads=["Fall", "small"], writes=["fqt"])
            S.op("dve", lambda e: e.scalar_tensor_tensor(fqt[:, tb, 0:NH], fb, nsel[:, j:j + 1], fqt[:, tb, 0:NH],
                                                         ALU.mult, ALU.add), reads=["Fall", "small", "fqt"], writes=["fqt"])
            S.op("dve", lambda e: e.tensor_tensor(fqt[:, tb, 0:NH], fqt[:, tb, 0:NH], Fref[:, j, :], ALU.subtract),
                 reads=["fqt", "Fref"], writes=["fqt"])
            S.op("dve", lambda e: e.tensor_copy(fqt[:, tb, 32:32 + NH], fqt[:, tb, 0:NH]), reads=["fqt"], writes=["fqt"])
        for tb in range(4):
            S.op("pe", lambda e: e.transpose(ps[7][0:48, tb * 128:(tb + 1) * 128], fqt[:, tb, :], ident_f),
                 reads=["fqt", "consts"], writes=[("ps", 7)], inc=(tb == 3))
        S.op("dve", lambda e: e.tensor_copy(fqh[0:48, :], ps[7][0:48, :]), reads=[("ps", 7)], writes=["fqh"])
        S.op("dve", lambda e: e.tensor_copy(fqhf[0:48, :], fqh[0:48, :]), reads=["fqh"], writes=["fqhf"])
        S.op("dve", lambda e: e.tensor_tensor(fql[0:48, :], ps[7][0:48, :], fqhf[0:48, :], ALU.subtract),
             reads=[("ps", 7), "fqhf"], writes=["fql"])
        S.op("dve", lambda e: e.tensor_copy(fqhl[0:16, :], fqh[0:16, :]), reads=["fqh", "fqhl"], writes=["fqhl"])
        S.op("dve", lambda e: e.tensor_copy(fqhl[32:48, :], fql[32:48, :]), reads=["fql", "fqhl"], writes=["fqhl"])
        S.op("dve", lambda e: e.tensor_tensor(biasK[:, 0:nkb, :], Fall[:, 0:nkb, :],
                                              Fref[:, j, :].unsqueeze(1).broadcast_to([128, nkb, NH]), ALU.subtract),
             reads=["Fall", "Fref"], writes=["biasK"])
        S.op("dve", lambda e: e.tensor_scalar(biasK[:, 0:nkb, :], biasK[:, 0:nkb, :], -1.0, None, ALU.mult),
             reads=["biasK"], writes=["biasK"])
        for mi in range(8):
            kb = nkb - 8 + mi
            S.op("dve", lambda e: e.tensor_scalar(mk, qposb, kpos[:, kb:kb + 1], None, ALU.is_ge),
                 reads=["qposb", "small"], writes=["mk"])
            S.op("dve", lambda e: e.tensor_scalar(maskadd[:, mi, :], mk, -1.0, 30000.0, ALU.add, ALU.mult),
                 reads=["mk"], writes=["maskadd"])
        for h in range(NH):
            hb = h % 2
            S.dma("sp", kst[hb][:, 0:nkb * 128], kT_d[h, :, 0:nkb * 128],
                  reads=[("kT", h, tt_) for tt_ in range(2 * j + 2)], writes=[("kst", hb)])
            S.dma("sp", vst[hb][:, 0:nkb, :], vS_d[h, :, 0:nkb, :],
                  reads=[("vS", h // 4, kk) for kk in range(nkb)], writes=[("vst", hb)])
            bO, bD = 3 + hb, 5 + hb

            def s_step(kb):
                sb_ = kb % 3
                S.op("pe", lambda e: e.matmul(ps[sb_][:, :], kst[hb][:, kb * 128:(kb + 1) * 128], QT[:, h, :],
                                              start=True, stop=False),
                     reads=[("kst", hb), ("QT", h)], writes=[("ps", sb_)], inc=False)
                S.op("pe", lambda e: e.matmul(ps[sb_][:, :], oneh[0:48, h, :], fqhl[0:48, :], start=False, stop=True),
                     reads=["oneh", "fqhl"], writes=[("ps", sb_)])
                pt = PT[sb_]
                if kb >= nkb - 8:
                    mi = kb - (nkb - 8)
                    m_ = mt[kb % 2]
                    S.op("dve", lambda e: e.scalar_tensor_tensor(m_, ps[sb_][:, :], biasK[:, kb, h:h + 1], maskadd[:, mi, :],
                                                                 ALU.add, ALU.add),
                         reads=[("ps", sb_), "biasK", "maskadd"], writes=[("mt", kb % 2)])
                    S.op("act", lambda e: e.activation(pt, m_, AF.Exp), reads=[("mt", kb % 2)], writes=[("PT", sb_)])
                else:
                    S.op("act", lambda e: e.activation(pt, ps[sb_][:, :], AF.Exp, bias=biasK[:, kb, h:h + 1]),
                         reads=[("ps", sb_), "biasK"], writes=[("PT", sb_)])

            def pv_step(kb):
                sb_ = kb % 3
                S.op("pe", lambda e: e.matmul(ps[bO][:, :], vst[hb][:, kb, :], PT[sb_], start=(kb == 0), stop=(kb == nkb - 1)),
                     reads=[("vst", hb), ("PT", sb_)], writes=[("ps", bO)], inc=False)
                S.op("pe", lambda e: e.matmul(ps[bD][:, :], ones_b, PT[sb_], start=(kb == 0), stop=(kb == nkb - 1)),
                     reads=["ones_b", ("PT", sb_)], writes=[("ps", bD)])
            for i in range(nkb + 2):
                if i < nkb:
                    s_step(i)
                if i >= 2:
                    pv_step(i - 2)
            S.op("dve", lambda e: e.reciprocal(rc, ps[bD][:, :]), reads=[("ps", bD)], writes=["rc"])
            S.op("dve", lambda e: e.tensor_tensor(attnT[:, h, :], ps[bO][:, :], rc, ALU.mult),
                 reads=[("ps", bO), "rc"], writes=[("attnT", h)])
        S.barrier()
        chk('attn')
        A.reset(m_a)
        ysT = A.alloc([128, SC, TT], BF16)
        gT_ = A.alloc([128, SC, TT], BF16)
        g1 = A.alloc([128, TT], F32)
        g2 = A.alloc([128, TT], F32)
        sig = A.alloc([128, TT], F32)
        gfT = A.alloc([128, 4, TT], BF16)
        gsT = A.alloc([128, 4, TT], BF16)
        mtmp = A.alloc([128, 4, TT], F32)
        mt2 = A.alloc([128, TT], F32)
        S.dma("sp", ysT, yS_d[j], reads=[("yS", j)], writes=["ysT"])
        for c in range(SC):
            yc = ysT[:, c, :]
            S.op("dve", lambda e: e.tensor_tensor(g1, yc, yc, ALU.mult), reads=["ysT"], writes=["g1"])
            S.op("dve", lambda e: e.tensor_scalar(g1, g1, 0.044715, 1.0, ALU.mult, ALU.add), reads=["g1"], writes=["g1"])
            S.op("dve", lambda e: e.tensor_tensor(g1, g1, yc, ALU.mult), reads=["g1", "ysT"], writes=["g1"])
            S.op("act", lambda e: e.activation(g2, g1, AF.Sigmoid, scale=2.0 * math.sqrt(2.0 / math.pi)), reads=["g1"],
                 writes=["g2"])
            S.op("dve", lambda e: e.tensor_tensor(gT_[:, c, :], g2, yc, ALU.mult),
                 reads=["g2", "ysT"], writes=[("gT", c)])

        def epi_glu(ct, bank):
            S.op("act", lambda e: e.activation(sig, ps[bank][:, :], AF.Sigmoid, bias=bglu_l[:, ct:ct + 1]),
                 reads=[("ps", bank), "small"], writes=["sig"])
            S.op("dve", lambda e: e.tensor_tensor(ssmT[:, ct, :], gT_[:, ct, :], sig, ALU.mult),
                 reads=["sig", ("gT", ct)], writes=[("ssmT", ct)])
        gemm_fm(gT_, "gT", SC, w_glu, 0, SW, epi_glu)
        gi = 0
        for cg in range(0, D, 512):
            c0t = cg // 128

            def epi_gf(ct, bank):
                S.op("act", lambda e: e.activation(gfT[:, ct, :], ps[bank][:, :], AF.Sigmoid, bias=bgT[:, c0t + ct:c0t + ct + 1]),
                     reads=[("ps", bank), "small"], writes=[("gfT", ct)])

            def epi_gs(ct, bank):
                S.op("act", lambda e: e.activation(gsT[:, ct, :], ps[bank][:, :], AF.Sigmoid,
                                                   bias=bgT[:, DC + c0t + ct:DC + c0t + ct + 1]),
                     reads=[("ps", bank), "small"], writes=[("gsT", ct)])

            def epi_pf(ct, bank):
                S.op("dve", lambda e: e.tensor_tensor(mtmp[:, ct, :], ps[bank][:, :], gfT[:, ct, :], ALU.mult),
                     reads=[("ps", bank), ("gfT", ct)], writes=[("mtmp", ct)])

            def epi_ps(ct, bank):
                S.op("dve", lambda e: e.tensor_tensor(mt2, ps[bank][:, :], gsT[:, ct, :], ALU.mult),
                     reads=[("ps", bank), ("gsT", ct)], writes=["mt2"])
                S.op("pool", lambda e: e.tensor_tensor(mergedT[:, c0t + ct, :], mt2, mtmp[:, ct, :], ALU.add),
                     reads=["mt2", ("mtmp", ct)], writes=[("mergedT", c0t + ct)])
            gi = gemm_fm(uT, "uT", DC, w_in, cfg.COL_GF + cg, 512, epi_gf, gi0=gi)
            gi = gemm_fm(uT, "uT", DC, w_in, cfg.COL_GS + cg, 512, epi_gs, gi0=gi)
            gi = gemm_fm(attnT, "attnT", NH, w_pf, cg, 512, epi_pf, gi0=gi)
            gi = gemm_fm(ssmT, "ssmT", SC, w_ps, cg, 512, epi_ps, gi0=gi)
        S.barrier()
        chk('B')
        A.reset(m_a)
        xres = [A.alloc([128, 512], F32) for _ in range(2)]
        htmp = [A.alloc([128, 512], F32) for _ in range(2)]
        junk = A.alloc([128, 512], BF16)
        h16 = A.alloc([128, 4, D], BF16)
        ssq2 = A.alloc([128, 4, 8], F32)
        NCB = D // 512
        S.op("dve", lambda e: e.memset(ssq2, 0.0), writes=["ssq2"])
        xi_ = [0]

        def epi_wo(cb, tb, bank, ncb):
            b_ = xi_[0] % 2
            xi_[0] += 1
            r0 = j * TT + tb * 128
            S.dma("sp", xres[b_], xo[r0:r0 + 128, cb * 512:(cb + 1) * 512], writes=[("xres", b_)])
            S.op("dve", lambda e: e.tensor_tensor(htmp[b_], ps[bank][:, :], xres[b_], ALU.add),
                 reads=[("ps", bank), ("xres", b_)], writes=[("htmp", b_)])
            S.dma("sp", hS_d[r0:r0 + 128, cb * 512:(cb + 1) * 512], htmp[b_], reads=[("htmp", b_)],
                  writes=[("hS", tb, cb)])
            S.op("act", lambda e: e.activation(h16[:, tb, cb * 512:(cb + 1) * 512], htmp[b_], AF.Copy),
                 reads=[("htmp", b_)], writes=[("h16", tb)])
            S.op("act", lambda e: e.activation(junk, htmp[b_], AF.Square, accum_out=ssq2[:, tb, cb:cb + 1]),
                 reads=[("htmp", b_), "ssq2"], writes=["junk", "ssq2"])
        gemm_tm(mergedT, "mergedT", DC, w_out, 0, D, epi_wo)
        for tb in range(4):
            ssq = stat[:, 0:1]
            S.op("dve", lambda e: e.reduce_sum(ssq, ssq2[:, tb, 0:NCB], AX.X), reads=["ssq2", "rs"], writes=["rs"])
            rstd_from(ssq, 1.0 / D)
            S.op("act", lambda e: e.activation(h16[:, tb, :], h16[:, tb, :], AF.Copy, scale=ssq),
                 reads=[("h16", tb), "rs"], writes=[("h16", tb)])
            transpose_to_uT(h16[:, tb, :], ("h16", tb), tb, gffnT, "small")
        S.barrier()
        chk('C')
        A.reset(m_base)
        actT = A.alloc([128, KF, TT], BF16)
        sgT = A.alloc([128, 4, TT], BF16)
        hres = [A.alloc([128, 512], F32) for _ in range(2)]
        ost = [A.alloc([128, 512], F32) for _ in range(2)]
        gi = 0
        for cg in range(0, DFF, 512):
            ncg = min(512, DFF - cg)
            c0t = cg // 128

            def epi_g(ct, bank):
                S.op("act", lambda e: e.activation(sgT[:, ct, :], ps[bank][:, :], AF.Silu), reads=[("ps", bank)],
                     writes=[("sgT", ct)])

            def epi_u(ct, bank):
                S.op("dve", lambda e: e.tensor_tensor(actT[:, c0t + ct, :], ps[bank][:, :], sgT[:, ct, :], ALU.mult),
                     reads=[("ps", bank), ("sgT", ct)], writes=[("actT", c0t + ct)])
            gi = gemm_fm(uT, "uT", DC, w_gu, cg, ncg, epi_g, gi0=gi)
            gi = gemm_fm(uT, "uT", DC, w_gu, DFF + cg, ncg, epi_u, gi0=gi)
        oi_ = [0]

        def epi_dn(cb, tb, bank, ncb):
            b_ = oi_[0] % 2
            oi_[0] += 1
            r0 = j * TT + tb * 128
            S.dma("sp", hres[b_], hS_d[r0:r0 + 128, cb * 512:(cb + 1) * 512], reads=[("hS", tb, cb)],
                  writes=[("hres", b_)])
            S.op("dve", lambda e: e.tensor_tensor(ost[b_], ps[bank][:, :], hres[b_], ALU.add),
                 reads=[("ps", bank), ("hres", b_)], writes=[("ost", b_)])
            S.dma("sp", y_d[r0:r0 + 128, cb * 512:(cb + 1) * 512], ost[b_], reads=[("ost", b_)], writes=[("y", r0, cb)])
        gemm_tm(actT, "actT", KF, w_dn, 0, D, epi_dn)
        S.barrier()
    S.barrier()
    info = dict(ops=S.n_ops, waits=S.n_wait, seq_peak=seq_peak, peak=A.peak)
    return nc, info


OWN_TILES = ((0, 3, 4, 7), (1, 2, 5, 6))


def make_in_maps(cfg, inp, n_pairs):
    D, NH, G, DFF = cfg.D, cfg.NH, cfg.G, cfg.DFF
    DC, SC, NGP = cfg.DC, cfg.SC, cfg.NGP
    f32 = np.float32

    def colT(v, n):
        return np.ascontiguousarray(np.asarray(v, f32).reshape(n, 128).T)

    def rep(v):
        v = np.asarray(v, f32)
        return np.ascontiguousarray(np.broadcast_to(v[None, :], (128, v.shape[0])))
    consts = np.zeros((128, 512), f32)
    consts[:, 0:128] = np.eye(128, dtype=f32)
    consts[:, 128:256] = np.triu(np.ones((128, 128), f32))
    consts[:, 256:384] = np.arange(1, 129, dtype=f32)[None, :]
    oneh = np.zeros((128, NH, 128), f32)
    for h in range(NH):
        oneh[h, h, :] = 1.0
        oneh[32 + h, h, :] = 1.0
    oneh = oneh.reshape(128, NH * 128).astype(ml_dtypes.bfloat16)
    kpos = (np.arange(NKB, dtype=f32)[None, :] * 128 + np.arange(128, dtype=f32)[:, None])

    def pairl(a):
        return np.asarray(a, f32).reshape(NGP, 2, 64).transpose(1, 2, 0).reshape(128, NGP)
    lam_re, lam_im = inp["s5_lambda_re"][0], inp["s5_lambda_im"][0]
    ls = np.broadcast_to(np.asarray(inp["s5_log_step"][0], f32)[:, None], (G, 64))
    bre = np.asarray(inp["s5_b_re"][0], f32).reshape(NGP, 2, 64, 16).transpose(1, 2, 0, 3).reshape(128, NGP * 16)
    bim = np.asarray(inp["s5_b_im"][0], f32).reshape(NGP, 2, 64, 16).transpose(1, 2, 0, 3).reshape(128, NGP * 16)
    cre = np.asarray(inp["s5_c_re"][0], f32).reshape(NGP, 2, 16, 64).transpose(1, 3, 0, 2).reshape(128, NGP * 16)
    cim = np.asarray(inp["s5_c_im"][0], f32).reshape(NGP, 2, 16, 64).transpose(1, 3, 0, 2).reshape(128, NGP * 16)
    s5p = np.ascontiguousarray(np.concatenate([pairl(lam_re), pairl(lam_im), pairl(ls), bre, bim, cre, cim], axis=1))
    shared = {
        "w_in": np.ascontiguousarray(inp["w_in"][0], dtype=f32),
        "w_glu": np.ascontiguousarray(inp["w_glu"][0], dtype=f32),
        "w_pf": np.ascontiguousarray(inp["w_proj_fox"][0], dtype=f32),
        "w_ps": np.ascontiguousarray(inp["w_proj_s5"][0], dtype=f32),
        "w_out": np.ascontiguousarray(inp["w_out"][0], dtype=f32),
        "w_gu": np.ascontiguousarray(inp["w_gate_up"][0], dtype=f32),
        "w_dn": np.ascontiguousarray(inp["w_down"][0], dtype=f32),
        "consts": consts, "oneh": oneh, "s5p": s5p,
    }
    in_maps = []
    x = np.asarray(inp["x"], f32)
    for c in range(2 * n_pairs):
        b, half = c // 2, c % 2
        own = OWN_TILES[half]
        selv = np.array([1.0 if own[j] == 2 * j else 0.0 for j in range(NJ)], f32)
        small = np.concatenate([
            colT(inp["g_mix"][0], DC), colT(inp["g_ffn"][0], DC), np.zeros((128, DC), f32),
            colT(inp["b_gates"][0], 2 * DC), colT(inp["q_norm"][0], 1), colT(inp["k_norm"][0], 1),
            rep(inp["b_fgate"][0]), rep(selv), rep(1.0 - selv), kpos,
            colT(np.asarray(inp["s5_d"][0], f32).reshape(-1), SC), colT(inp["b_glu"][0], SC)], axis=1)
        qpos = np.concatenate([np.arange(t * TT, (t + 1) * TT, dtype=f32) for t in own])
        m = dict(shared)
        m["xs"] = np.ascontiguousarray(x[b])
        m["xo"] = np.ascontiguousarray(np.concatenate([x[b, t * TT:(t + 1) * TT] for t in own], axis=0))
        m["small"] = np.ascontiguousarray(small, dtype=f32)
        m["qpos"] = np.ascontiguousarray(np.broadcast_to(qpos[None, :], (128, NJ * TT)), dtype=f32)
        in_maps.append(m)
    return in_maps


def gather(cfg, results, n_pairs):
    out = np.zeros((n_pairs, SEQ, cfg.D), np.float32)
    for c in range(2 * n_pairs):
        b, half = c // 2, c % 2
        y = results[c]["y"]
        for j, t in enumerate(OWN_TILES[half]):
            out[b, t * TT:(t + 1) * TT] = y[j * TT:(j + 1) * TT]
    return out


def kernel(**inputs):
    cfg = Cfg()
    nc, info = build(cfg)
    in_maps = make_in_maps(cfg, inputs, 4)
    res = run_bass_kernel_spmd(nc, in_maps, core_ids=list(range(8)))
    return gather(cfg, res.results, 4)
```

```python
import math
import numpy as np
import ml_dtypes
import concourse.bass as bass
import concourse.mybir as mybir
from concourse.bass_utils import run_bass_kernel_spmd

F32 = mybir.dt.float32
BF16 = mybir.dt.bfloat16
U8 = mybir.dt.uint8
AF = mybir.ActivationFunctionType
ALU = mybir.AluOpType
AX = mybir.AxisListType
EPS = 1e-6
SEQ = 4096
TT = 512
NT = SEQ // TT
NJ = NT // 2
NKB = SEQ // 128
MAGIC = 12582912.0
TWO_PI = 2.0 * math.pi


class Cfg:
    def __init__(self, D=4096, NH=16, G=64, DFF=11008):
        self.D, self.NH, self.G, self.DFF = D, NH, G, DFF
        self.DC = D // 128
        self.FW = NH * 128
        self.SW = G * 16
        self.SC = self.SW // 128
        self.NGP = G // 2
        self.KF = DFF // 128
        self.COL_K = self.FW
        self.COL_V = 2 * self.FW
        self.COL_F = 3 * self.FW
        self.COL_S5 = 3 * self.FW + NH
        self.COL_GF = self.COL_S5 + self.SW
        self.COL_GS = self.COL_GF + D
        self.INC = self.COL_GS + D


class Sched:
    def __init__(self, nc):
        self.nc = nc
        self.eng = {"pe": nc.tensor, "act": nc.scalar, "dve": nc.vector,
                    "pool": nc.gpsimd, "sp": nc.sync}
        self.sems = {}
        self.count = {}
        for n in ("pe", "act", "dve", "pool"):
            self.sems[n] = nc.alloc_semaphore("s_" + n)
            self.count[n] = 0
        self.rings = {"sp": [], "pool": []}
        for q, n in (("sp", 40), ("pool", 12)):
            for i in range(n):
                nm = f"dq_{q}{i}"
                self.sems[nm] = nc.alloc_semaphore("s_" + nm)
                self.count[nm] = 0
                self.rings[q].append(nm)
        self.dma_idx = {"sp": 0, "pool": 0}
        self.waited = {e: {} for e in self.eng}
        self.last_w = {}
        self.readers = {}
        self.n_wait = 0
        self.n_ops = 0

    def _need(self, eng, reads, writes, is_dma):
        need = {}

        def add(tok, raw):
            if tok is None:
                return
            s, v, ex = tok
            if (not is_dma) and ex == eng:
                if not raw or eng == "pe":
                    return
            if self.waited[eng].get(s, 0) >= v:
                return
            if need.get(s, 0) < v:
                need[s] = v
        for r in reads:
            add(self.last_w.get(r), True)
        for w in writes:
            add(self.last_w.get(w), False)
            for s, (v, ex) in self.readers.get(w, {}).items():
                add((s, v, ex), False)
        return need

    def _emit_waits(self, eng, need):
        e = self.eng[eng]
        for s, v in need.items():
            e.wait_ge(self.sems[s], v)
            self.waited[eng][s] = v
            self.n_wait += 1

    def _record(self, tok, reads, writes):
        s, v, ex = tok
        for w in writes:
            self.last_w[w] = tok
            self.readers[w] = {}
        for r in reads:
            self.readers.setdefault(r, {})[s] = (v, ex)

    def op(self, eng, fn, reads=(), writes=(), inc=True):
        self._emit_waits(eng, self._need(eng, reads, writes, False))
        ins = fn(self.eng[eng])
        self.n_ops += 1
        if inc:
            self.count[eng] += 1
            ins.then_inc(self.sems[eng], 1)
            tok = (eng, self.count[eng], eng)
        else:
            tok = (eng, self.count[eng] + 1, eng)
        self._record(tok, reads, writes)
        return ins

    def dma(self, q, out, in_, reads=(), writes=()):
        ring = self.rings[q]
        sname = ring[self.dma_idx[q] % len(ring)]
        self.dma_idx[q] += 1
        need = self._need(q, reads, writes, True)
        prev = self.count[sname]
        if prev > 0 and self.waited[q].get(sname, 0) < prev:
            need[sname] = max(need.get(sname, 0), prev)
        self._emit_waits(q, need)
        ins = self.eng[q].dma_start(out=out, in_=in_)
        self.count[sname] += 16
        ins.then_inc(self.sems[sname], 16)
        self._record((sname, self.count[sname], "dma"), reads, writes)
        self.n_ops += 1
        return ins

    def barrier(self, engines=None):
        for en in (engines or list(self.eng)):
            e = self.eng[en]
            for s, c in self.count.items():
                if c > 0 and self.waited[en].get(s, 0) < c:
                    e.wait_ge(self.sems[s], c)
                    self.waited[en][s] = c
                    self.n_wait += 1


class Arena:
    def __init__(self, nc, nbytes):
        self.t = nc.alloc_sbuf_tensor("arena", [128, nbytes], U8)
        self.nbytes = nbytes
        self.off = 0
        self.peak = 0

    def mark(self):
        return self.off

    def reset(self, m):
        self.off = m

    def alloc(self, shape, dt):
        esz = 4 if dt == F32 else 2
        n = esz
        for s in shape[1:]:
            n *= s
        self.off = (self.off + 63) // 64 * 64
        assert self.off + n <= self.nbytes, ("arena overflow", self.off, n, self.nbytes)
        ap = self.t[:, self.off:self.off + n].bitcast(dt)
        self.off += n
        self.peak = max(self.peak, self.off)
        if len(shape) == 3:
            ap = ap.rearrange("p (a b) -> p a b", a=shape[1])
        elif len(shape) == 4:
            ap = ap.rearrange("p (a b c) -> p a b c", a=shape[1], b=shape[2])
        return ap


class _Stop(Exception):
    pass


def build(cfg, stop=None):
    try:
        return _build(cfg, stop)
    except _Stop as ex:
        nc, S, A = ex.args
        S.barrier()
        return nc, dict(ops=S.n_ops, waits=S.n_wait, peak=A.peak, stopped=stop)


def _build(cfg, stop):
    D, NH, G, DFF = cfg.D, cfg.NH, cfg.G, cfg.DFF
    DC, FW, SW, SC, NGP, KF = cfg.DC, cfg.FW, cfg.SW, cfg.SC, cfg.NGP, cfg.KF
    nc = bass.Bass("TRN2", target_bir_lowering=False)

    def din(name, shape, dt=F32):
        return nc.dram_tensor(name, list(shape), dt, kind="ExternalInput").ap()

    xs = din("xs", [SEQ, D])
    xo = din("xo", [NJ * TT, D])
    w_in = din("w_in", [D, cfg.INC])
    w_glu = din("w_glu", [SW, SW])
    w_pf = din("w_pf", [FW, D])
    w_ps = din("w_ps", [SW, D])
    w_out = din("w_out", [D, D])
    w_gu = din("w_gu", [D, 2 * DFF])
    w_dn = din("w_dn", [DFF, D])
    NSM = 3 * DC + 2 * DC + 2 + NH + 8 + NKB + 2 * SC
    small_d = din("small", [128, NSM])
    qpos_d = din("qpos", [128, NJ * TT])
    consts_d = din("consts", [128, 4 * 128])
    oneh_d = din("oneh", [128, NH * 128], BF16)
    s5p_d = din("s5p", [128, 3 * NGP + 4 * NGP * 16])
    y_d = nc.dram_tensor("y", [NJ * TT, D], F32, kind="ExternalOutput").ap()
    kT_d = nc.dram_tensor("kT_s", [NH, 128, SEQ], BF16, kind="Internal").ap()
    vS_d = nc.dram_tensor("vS_s", [NH, 128, NKB, 128], BF16, kind="Internal").ap()
    yS_d = nc.dram_tensor("yS_s", [NJ, 128, SC, TT], BF16, kind="Internal").ap()
    hS_d = nc.dram_tensor("hS_s", [NJ * TT, D], F32, kind="Internal").ap()
    wtot = D * cfg.INC + SW * SW + FW * D + SW * D + D * D + D * 2 * DFF + DFF * D
    NSLOT = wtot // (128 * 8 * 512) + 64
    SPT = 192
    wc_list = [nc.dram_tensor(f"wc_s{i}", [SPT, 128, 8 * 512], BF16, kind="Internal").ap()
               for i in range((NSLOT + SPT - 1) // SPT)]

    class _WC:
        def __getitem__(self, idx):
            slot = idx[0]
            return wc_list[slot // SPT][(slot % SPT,) + tuple(idx[1:])]
    wc_d = _WC()

    S = Sched(nc)
    A = Arena(nc, 206 * 1024)

    def chk(name):
        if stop == name:
            raise _Stop(nc, S, A)
    ps = [nc.alloc_psum_tensor(f"ps{i}", [128, 512], F32) for i in range(8)]

    small = A.alloc([128, NSM], F32)
    consts = A.alloc([128, 512], F32)
    oneh = A.alloc([128, NH, 128], BF16)
    ident_f = consts[:, 0:128]
    tri_f = consts[:, 128:256]
    iota1 = consts[:, 256:384]
    ident_b = A.alloc([128, 128], BF16)
    ones_b = A.alloc([128, 128], BF16)
    ones_f = A.alloc([128, 128], F32)
    qn_s = A.alloc([128, 1], F32)
    Fall = A.alloc([128, NKB, NH], F32)
    carry = A.alloc([128, NH], F32)
    Fref = A.alloc([128, NJ, NH], F32)
    stat = A.alloc([128, 64], F32)
    eps_col = stat[:, 32:33]
    o = 0
    gmixT = small[:, o:o + DC]; o += DC
    gffnT = small[:, o:o + DC]; o += DC
    o += DC
    bgT = small[:, o:o + 2 * DC]; o += 2 * DC
    qn_c = small[:, o:o + 1]; o += 1
    kn_c = small[:, o:o + 1]; o += 1
    bf_rep = small[:, o:o + NH]; o += NH
    sel = small[:, o:o + 4]; o += 4
    nsel = small[:, o:o + 4]; o += 4
    kpos = small[:, o:o + NKB]; o += NKB
    d_l = small[:, o:o + SC]; o += SC
    bglu_l = small[:, o:o + SC]; o += SC
    assert o == NSM
    S.dma("sp", small, small_d, writes=["small"])
    S.dma("sp", consts, consts_d, writes=["consts"])
    S.dma("sp", oneh.rearrange("p a b -> p (a b)"), oneh_d, writes=["oneh"])
    S.op("dve", lambda e: e.tensor_copy(ident_b, ident_f), reads=["consts"], writes=["ident_b"])
    S.op("dve", lambda e: e.memset(ones_b, 1.0), writes=["ones_b"])
    S.op("dve", lambda e: e.memset(ones_f, 1.0), writes=["ones_f"])
    S.op("dve", lambda e: e.memset(carry, 0.0), writes=["carry"])
    S.op("dve", lambda e: e.memset(eps_col, EPS), writes=["eps_col"])
    S.op("dve", lambda e: e.tensor_scalar(qn_s, qn_c, 1.0 / math.sqrt(128.0), None, ALU.mult),
         reads=["small"], writes=["qn_s"])

    uT = A.alloc([128, DC, TT], BF16)
    NWT = 3
    KG = 8
    wt = [A.alloc([128, KG, 512], BF16) for _ in range(NWT)]
    wt_i = [0]
    m_base = A.mark()

    wc_slots = {}
    nfirst = [0]

    def load_w(w_ap, k0, kg, c0, ncl):
        b = wt_i[0]
        wt_i[0] = (b + 1) % NWT
        key = (w_ap.name, k0, c0, ncl)
        dst = wt[b][:, 0:kg, 0:ncl]
        if key in wc_slots:
            slot = wc_slots[key]
            S.dma("pool", dst, wc_d[slot, :, 0:kg * ncl].rearrange("p (k n) -> p k n", k=kg),
                  reads=[("wc", slot)], writes=[("wt", b)])
        else:
            src = w_ap[k0 * 128:(k0 + kg) * 128, c0:c0 + ncl].rearrange("(kc p) n -> p kc n", p=128)
            S.dma("pool", dst, src, writes=[("wt", b)])
            nfirst[0] += 1
            seqw = (w_ap.name == "w_in" and cfg.COL_K <= c0 < cfg.COL_GF)
            if seqw or nfirst[0] % 2 == 0:
                slot = len(wc_slots)
                assert slot < NSLOT
                wc_slots[key] = slot
                S.dma("sp", wc_d[slot, :, 0:kg * ncl].rearrange("p (k n) -> p k n", k=kg), dst,
                      reads=[("wt", b)], writes=[("wc", slot)])
        return b

    def gemm_fm(actT, ares, KC, w_ap, c0, ncols, epi, gcols=512, bank_sets=((0, 1, 2, 3), (4, 5, 6, 7)),
                gi0=0):
        gi = gi0
        for cg0 in range(c0, c0 + ncols, gcols):
            ncg = min(gcols, c0 + ncols - cg0)
            nct = ncg // 128
            banks = bank_sets[gi % len(bank_sets)]
            gi += 1
            for k0 in range(0, KC, KG):
                kg = min(KG, KC - k0)
                b = load_w(w_ap, k0, kg, cg0, ncg)
                for ct in range(nct):
                    for kc in range(kg):
                        S.op("pe", lambda e: e.matmul(ps[banks[ct]][:, :], wt[b][:, kc, ct * 128:(ct + 1) * 128],
                                                      actT[:, k0 + kc, :], start=(k0 + kc == 0),
                                                      stop=(k0 + kc == KC - 1)),
                             reads=[("wt", b), (ares, k0 + kc)], writes=[("ps", banks[ct])], inc=(kc == kg - 1))
                if tick[0] is not None and tick_on[0]:
                    tick[0]()
            for ct in range(nct):
                epi((cg0 - c0) // 128 + ct, banks[ct])
        return gi

    def gemm_tm(actT, ares, KC, w_ap, c0, ncols, epi, bank_sets=((0, 1, 2, 3), (4, 5, 6, 7))):
        gi = 0
        for cb0 in range(c0, c0 + ncols, 512):
            ncb = min(512, c0 + ncols - cb0)
            banks = bank_sets[gi % len(bank_sets)]
            for k0 in range(0, KC, KG):
                kg = min(KG, KC - k0)
                b = load_w(w_ap, k0, kg, cb0, ncb)
                for tb in range(4):
                    for kc in range(kg):
                        S.op("pe", lambda e: e.matmul(ps[banks[tb]][:, 0:ncb], actT[:, k0 + kc, tb * 128:(tb + 1) * 128],
                                                      wt[b][:, kc, 0:ncb], start=(k0 + kc == 0),
                                                      stop=(k0 + kc == KC - 1)),
                             reads=[("wt", b), (ares, k0 + kc)], writes=[("ps", banks[tb])], inc=(kc == kg - 1))
                if tick[0] is not None and tick_on[0]:
                    tick[0]()
            for tb in range(4):
                epi(gi, tb, banks[tb], ncb)
            gi += 1

    import os as _os
    DBG = _os.environ.get("KDBG", "")
    tr_i = [0]
    TRB = [6, 7]
    NRB = [6, 7]
    TR_ALL_ACT = [False]
    tick = [None]
    tick_on = [True]

    def transpose_to_uT(src, sres, tb, gT, gres):
        for c8 in range(0, DC, 8):
            n8 = min(8, DC - c8)
            bank = TRB[tr_i[0] % 2]
            tr_i[0] += 1
            pb = ps[bank][:, :].bitcast(BF16)
            for i in range(n8):
                S.op("pe", lambda e: e.transpose(pb[:, i * 128:(i + 1) * 128], src[:, (c8 + i) * 128:(c8 + i + 1) * 128],
                                                 ident_b),
                     reads=[sres, "ident_b"], writes=[("ps", bank)], inc=(i == n8 - 1))
            for i in range(n8):
                c = c8 + i
                dst = uT[:, c, tb * 128:(tb + 1) * 128]
                if "e" in DBG:
                    continue
                if "g" in DBG:
                    S.op("dve", lambda e: e.tensor_copy(dst, pb[:, i * 128:(i + 1) * 128]),
                         reads=[("ps", bank), gres], writes=[("uT", c)])
                    continue
                if bank == TRB[0] or TR_ALL_ACT[0]:
                    S.op("act", lambda e: e.activation(dst, pb[:, i * 128:(i + 1) * 128], AF.Copy, scale=gT[:, c:c + 1]),
                         reads=[("ps", bank), gres], writes=[("uT", c)])
                else:
                    S.op("dve", lambda e: e.tensor_scalar(dst, pb[:, i * 128:(i + 1) * 128], gT[:, c:c + 1], None, ALU.mult),
                         reads=[("ps", bank), gres], writes=[("uT", c)])

    def rstd_from(col, scale):
        S.op("dve", lambda e: e.tensor_scalar(col, col, scale, EPS, ALU.mult, ALU.add), reads=["rs"], writes=["rs"])
        S.op("act", lambda e: e.activation(col, col, AF.Sqrt), reads=["rs"], writes=["rs"])
        S.op("dve", lambda e: e.reciprocal(col, col), reads=["rs"], writes=["rs"])

    def norm1(x_d, r0, xst, xn):
        ssq = stat[:, 0:1]
        for tb in range(4):
            S.dma("sp", xst, x_d[r0 + tb * 128:r0 + (tb + 1) * 128, :], writes=["xst"])
            S.op("dve", lambda e: e.memset(ssq, 0.0), reads=["rs"], writes=["rs"])
            if "a" not in DBG:
                S.op("act", lambda e: e.activation(xn, xst, AF.Square, accum_out=ssq), reads=["xst", "rs"],
                     writes=["xn", "rs"])
            if "b" not in DBG:
                rstd_from(ssq, 1.0 / D)
            if "c" not in DBG:
                S.op("dve", lambda e: e.tensor_scalar(xn, xst, ssq, None, ALU.mult), reads=["xst", "rs"], writes=["xn"])
            if "d" not in DBG:
                transpose_to_uT(xn, "xn", tb, gmixT, "small")

    nrm_i = [0]

    def qknorm(bank, gain_col, gres, out_ap, out_res, tmp):
        sqs, rstds = tmp
        i_ = nrm_i[0] % 2
        nb = NRB[nrm_i[0] % len(NRB)]
        nrm_i[0] += 1
        sq, rstd = sqs[i_], rstds[i_]
        S.op("act", lambda e: e.activation(sq, ps[bank][:, :], AF.Square), reads=[("ps", bank)], writes=[("sq", i_)])
        S.op("pe", lambda e: e.matmul(ps[nb][:, :], ones_b, sq, start=True, stop=True),
             reads=["ones_b", ("sq", i_)], writes=[("ps", nb)])
        S.op("act", lambda e: e.activation(rstd, ps[nb][:, :], AF.Ln, bias=eps_col, scale=1.0 / 128.0),
             reads=[("ps", nb), "eps_col"], writes=[("rstd", i_)])
        S.op("act", lambda e: e.activation(rstd, rstd, AF.Exp, scale=-0.5), reads=[("rstd", i_)], writes=[("rstd", i_)])
        S.op("dve", lambda e: e.scalar_tensor_tensor(out_ap, ps[bank][:, :], gain_col, rstd, ALU.mult, ALU.mult),
             reads=[("ps", bank), ("rstd", i_), gres], writes=[out_res])

    def range_reduce(t):
        tmp = rr_tmp[:, 0:t.shape[-1]] if len(t.shape) == 2 else None
        S.op("dve", lambda e: e.tensor_scalar(tmp, t, 1.0 / TWO_PI, MAGIC, ALU.mult, ALU.add), reads=["s5c"], writes=["s5c"])
        S.op("dve", lambda e: e.tensor_scalar(tmp, tmp, MAGIC, -TWO_PI, ALU.subtract, ALU.mult), reads=["s5c"], writes=["s5c"])
        S.op("dve", lambda e: e.tensor_tensor(t, t, tmp, ALU.add), reads=["s5c"], writes=["s5c"])

    xst = A.alloc([128, D], F32)
    xn = A.alloc([128, D], BF16)
    Blhs = A.alloc([128, NGP, 2, 128], BF16)
    Clhs = A.alloc([128, NGP, 2, 128], BF16)
    ctab = A.alloc([128, NGP, 128], BF16)
    stab = A.alloc([128, NGP, 128], BF16)
    sc = A.alloc([128, 16, NGP], F32)
    hst = A.alloc([128, NGP, 2], F32)
    m_setup = A.mark()
    s5p = A.alloc([128, 3 * NGP + 4 * NGP * 16], F32)
    rr_tmp = A.alloc([128, 128], F32)
    ang_t = A.alloc([128, 128], F32)
    zt = A.alloc([128, 128], F32)
    bb = A.alloc([128, 2, NGP, 16], F32)

    S.dma("sp", s5p, s5p_d, writes=["s5p"])
    lr = s5p[:, 0:NGP]
    li = s5p[:, NGP:2 * NGP]
    lsx = s5p[:, 2 * NGP:3 * NGP]
    o = 3 * NGP
    bre = s5p[:, o:o + NGP * 16].rearrange("p (a b) -> p a b", a=NGP); o += NGP * 16
    bim = s5p[:, o:o + NGP * 16].rearrange("p (a b) -> p a b", a=NGP); o += NGP * 16
    cre = s5p[:, o:o + NGP * 16].rearrange("p (a b) -> p a b", a=NGP); o += NGP * 16
    cim = s5p[:, o:o + NGP * 16].rearrange("p (a b) -> p a b", a=NGP); o += NGP * 16
    dt_, mag, ang, lbr, lbi, den, nre, facr, faci, c128, s128, tA, tB, tC = [sc[:, i, :] for i in range(14)]

    def s5op(eng, fn):
        S.op(eng, fn, reads=["s5p", "s5c"], writes=["s5c"])
    s5op("act", lambda e: e.activation(dt_, lsx, AF.Exp))
    s5op("dve", lambda e: e.tensor_tensor(tA, lr, dt_, ALU.mult))
    s5op("act", lambda e: e.activation(mag, tA, AF.Exp))
    s5op("dve", lambda e: e.tensor_tensor(ang, li, dt_, ALU.mult))
    s5op("dve", lambda e: e.tensor_copy(tA, ang))
    range_reduce(tA)
    s5op("act", lambda e: e.activation(tB, tA, AF.Sin))
    s5op("dve", lambda e: e.tensor_scalar(tA, ang, math.pi / 2, None, ALU.add))
    range_reduce(tA)
    s5op("act", lambda e: e.activation(tC, tA, AF.Sin))
    s5op("dve", lambda e: e.tensor_tensor(lbr, mag, tC, ALU.mult))
    s5op("dve", lambda e: e.tensor_tensor(lbi, mag, tB, ALU.mult))
    s5op("dve", lambda e: e.tensor_scalar(tA, ang, 128.0, None, ALU.mult))
    range_reduce(tA)
    s5op("act", lambda e: e.activation(s128, tA, AF.Sin))
    s5op("dve", lambda e: e.tensor_scalar(tA, ang, 128.0, math.pi / 2, ALU.mult, ALU.add))
    range_reduce(tA)
    s5op("act", lambda e: e.activation(c128, tA, AF.Sin))
    s5op("dve", lambda e: e.tensor_tensor(den, lr, lr, ALU.mult))
    s5op("dve", lambda e: e.tensor_tensor(tA, li, li, ALU.mult))
    s5op("dve", lambda e: e.tensor_tensor(den, den, tA, ALU.add))
    s5op("dve", lambda e: e.reciprocal(den, den))
    s5op("dve", lambda e: e.tensor_scalar(nre, lbr, -1.0, None, ALU.add))
    s5op("dve", lambda e: e.tensor_tensor(tA, nre, lr, ALU.mult))
    s5op("dve", lambda e: e.tensor_tensor(tB, lbi, li, ALU.mult))
    s5op("dve", lambda e: e.tensor_tensor(tA, tA, tB, ALU.add))
    s5op("dve", lambda e: e.tensor_tensor(facr, tA, den, ALU.mult))
    s5op("dve", lambda e: e.tensor_tensor(tA, lbi, lr, ALU.mult))
    s5op("dve", lambda e: e.tensor_tensor(tB, nre, li, ALU.mult))
    s5op("dve", lambda e: e.tensor_tensor(tA, tA, tB, ALU.subtract))
    s5op("dve", lambda e: e.tensor_tensor(faci, tA, den, ALU.mult))
    s5op("dve", lambda e: e.memset(hst, 0.0))
    s5op("dve", lambda e: e.memset(Blhs, 0.0))
    s5op("dve", lambda e: e.memset(Clhs, 0.0))
    for gp in range(NGP):
        q4 = gp % 4
        s5op("dve", lambda e: e.tensor_scalar(bb[:, 0, gp, :], bim[:, gp, :], faci[:, gp:gp + 1], None, ALU.mult))
        s5op("dve", lambda e: e.scalar_tensor_tensor(bb[:, 0, gp, :], bre[:, gp, :], facr[:, gp:gp + 1], bb[:, 0, gp, :],
                                                     ALU.mult, ALU.subtract))
        s5op("dve", lambda e: e.tensor_scalar(bb[:, 1, gp, :], bre[:, gp, :], faci[:, gp:gp + 1], None, ALU.mult))
        s5op("dve", lambda e: e.scalar_tensor_tensor(bb[:, 1, gp, :], bim[:, gp, :], facr[:, gp:gp + 1], bb[:, 1, gp, :],
                                                     ALU.mult, ALU.add))
        for ri in range(2):
            s5op("dve", lambda e: e.memset(zt, 0.0))
            s5op("dve", lambda e: e.tensor_copy(zt[0:64, q4 * 32:q4 * 32 + 16], bb[0:64, ri, gp, :]))
            s5op("dve", lambda e: e.tensor_copy(zt[64:128, q4 * 32 + 16:q4 * 32 + 32], bb[64:128, ri, gp, :]))
            S.op("pe", lambda e: e.transpose(ps[0][:, 0:128], zt, ident_f), reads=["s5c", "consts"], writes=[("ps", 0)])
            S.op("act", lambda e: e.activation(Blhs[:, gp, ri, :], ps[0][:, 0:128], AF.Copy), reads=[("ps", 0)],
                 writes=["s5c"])
        s5op("dve", lambda e: e.tensor_copy(Clhs[0:64, gp, 0, q4 * 32:q4 * 32 + 16], cre[0:64, gp, :]))
        s5op("dve", lambda e: e.tensor_copy(Clhs[64:128, gp, 0, q4 * 32 + 16:q4 * 32 + 32], cre[64:128, gp, :]))
        s5op("dve", lambda e: e.tensor_scalar(Clhs[0:64, gp, 1, q4 * 32:q4 * 32 + 16], cim[0:64, gp, :], -1.0, None, ALU.mult))
        s5op("dve", lambda e: e.tensor_scalar(Clhs[64:128, gp, 1, q4 * 32 + 16:q4 * 32 + 32], cim[64:128, gp, :], -1.0, None,
                                              ALU.mult))
        s5op("dve", lambda e: e.tensor_scalar(ang_t, iota1, ang[:, gp:gp + 1], None, ALU.mult))
        range_reduce(ang_t)
        s5op("act", lambda e: e.activation(stab[:, gp, :], ang_t, AF.Sin))
        s5op("dve", lambda e: e.tensor_scalar(ang_t, iota1, ang[:, gp:gp + 1], math.pi / 2, ALU.mult, ALU.add))
        range_reduce(ang_t)
        s5op("act", lambda e: e.activation(ctab[:, gp, :], ang_t, AF.Sin))

    S.barrier()
    A.reset(m_setup)
    s5u2 = [A.alloc([128, SC, TT], BF16)]

    class _TS:
        pass
    tsets = []
    for _k in range(2):
        T_ = _TS()
        for nm in ("bur", "bui", "t1", "t2", "t3", "t4", "wre", "wim"):
            setattr(T_, nm, A.alloc([128, TT], F32))
        T_.cini = A.alloc([128, 8], F32)
        tsets.append(T_)
    xr_b2 = [A.alloc([128, TT], BF16) for _ in range(2)]
    xi_b2 = [A.alloc([128, TT], BF16) for _ in range(2)]
    ysel = A.alloc([128, SC, TT], BF16)
    yt = A.alloc([128, TT], F32)
    sqb = [A.alloc([128, TT], BF16) for _ in range(2)]
    rstd = [A.alloc([128, TT], F32) for _ in range(2)]
    kout = [A.alloc([128, TT], BF16) for _ in range(2)]
    vstg = [A.alloc([128, 512], BF16) for _ in range(2)]
    ftmp = A.alloc([128, 4, NH], F32)

    chk('setup')

    def bc4(tab_gp):
        return tab_gp.unsqueeze(1).broadcast_to([128, 4, 128])

    def v4(ap):
        return ap.rearrange("p (a b) -> p a b", a=4)

    def s5_front_gen(t, gp, j):
        k = gp % 2
        T = tsets[k]
        c = gp // 4
        bk = (0, 1) if k == 0 else (2, 3)
        s5u = s5u2[0]
        sres = ("s5u", 0, c)
        xr_b, xi_b = xr_b2[k], xi_b2[k]

        def R(n):
            return (n, k)
        S.op("pe", lambda e: e.matmul(ps[bk[0]][:, :], Blhs[:, gp, 0, :], s5u[:, c, :], start=True, stop=True),
             reads=["s5c", sres], writes=[("ps", bk[0])])
        S.op("pe", lambda e: e.matmul(ps[bk[1]][:, :], Blhs[:, gp, 1, :], s5u[:, c, :], start=True, stop=True),
             reads=["s5c", sres], writes=[("ps", bk[1])])
        S.op("act", lambda e: e.activation(T.bur, ps[bk[0]][:, :], AF.Copy), reads=[("ps", bk[0])], writes=[R("bur")])
        S.op("act", lambda e: e.activation(T.bui, ps[bk[1]][:, :], AF.Copy), reads=[("ps", bk[1])], writes=[R("bui")])
        yield
        ct4, st4 = bc4(ctab[:, gp, :]), bc4(stab[:, gp, :])
        S.op("dve", lambda e: e.tensor_tensor(v4(T.t1), v4(T.bur), ct4, ALU.mult), reads=[R("bur"), "s5c"], writes=[R("t1")])
        S.op("pool", lambda e: e.tensor_tensor(v4(T.t3), v4(T.bui), ct4, ALU.mult), reads=[R("bui"), "s5c"], writes=[R("t3")])
        yield
        S.op("dve", lambda e: e.tensor_tensor(v4(T.t2), v4(T.bui), st4, ALU.mult), reads=[R("bui"), "s5c"], writes=[R("t2")])
        S.op("pool", lambda e: e.tensor_tensor(v4(T.t4), v4(T.bur), st4, ALU.mult), reads=[R("bur"), "s5c"], writes=[R("t4")])
        yield
        S.op("dve", lambda e: e.tensor_tensor(T.t1, T.t1, T.t2, ALU.add), reads=[R("t1"), R("t2")], writes=[R("t1")])
        S.op("pool", lambda e: e.tensor_tensor(T.t3, T.t3, T.t4, ALU.subtract), reads=[R("t3"), R("t4")], writes=[R("t3")])
        yield "mid"
        rb = mag[:, gp:gp + 1].broadcast_to([128, 128])
        cini = T.cini
        for sb in range(4):
            ir = hst[:, gp, 0:1] if sb == 0 else cini[:, 0:1]
            ii = hst[:, gp, 1:2] if sb == 0 else cini[:, 1:2]
            sl = slice(sb * 128, (sb + 1) * 128)
            S.op("dve", lambda e: e.tensor_tensor_scan(T.wre[:, sl], rb, T.t1[:, sl], ir, ALU.mult, ALU.add),
                 reads=[R("t1"), "s5c", R("cini"), ("hst", gp)], writes=[R("wre")])
            S.op("dve", lambda e: e.tensor_tensor_scan(T.wim[:, sl], rb, T.t3[:, sl], ii, ALU.mult, ALU.add),
                 reads=[R("t3"), "s5c", R("cini"), ("hst", gp)], writes=[R("wim")])
            yield
            lr_, li_ = T.wre[:, sb * 128 + 127:sb * 128 + 128], T.wim[:, sb * 128 + 127:sb * 128 + 128]
            dr = hst[:, gp, 0:1] if sb == 3 else cini[:, 0:1]
            di = hst[:, gp, 1:2] if sb == 3 else cini[:, 1:2]
            dres = ("hst", gp) if sb == 3 else R("cini")
            S.op("dve", lambda e: e.tensor_scalar(cini[:, 2:3], li_, s128[:, gp:gp + 1], None, ALU.mult),
                 reads=[R("wim"), "s5c"], writes=[R("cini2")])
            S.op("dve", lambda e: e.tensor_scalar(cini[:, 3:4], lr_, s128[:, gp:gp + 1], None, ALU.mult),
                 reads=[R("wre"), "s5c"], writes=[R("cini2")])
            yield
            S.op("dve", lambda e: e.scalar_tensor_tensor(dr, lr_, c128[:, gp:gp + 1], cini[:, 2:3], ALU.mult, ALU.subtract),
                 reads=[R("wre"), R("cini2"), "s5c"], writes=[dres])
            S.op("dve", lambda e: e.scalar_tensor_tensor(di, li_, c128[:, gp:gp + 1], cini[:, 3:4], ALU.mult, ALU.add),
                 reads=[R("wim"), R("cini2"), "s5c"], writes=[dres])
            yield
        S.op("pool", lambda e: e.tensor_tensor(v4(T.bur), v4(T.wre), ct4, ALU.mult), reads=[R("wre"), "s5c"], writes=[R("bur")])
        S.op("dve", lambda e: e.tensor_tensor(v4(T.t2), v4(T.wim), st4, ALU.mult), reads=[R("wim"), "s5c"], writes=[R("t2")])
        yield
        S.op("pool", lambda e: e.tensor_tensor(v4(T.bui), v4(T.wim), ct4, ALU.mult), reads=[R("wim"), "s5c"], writes=[R("bui")])
        S.op("dve", lambda e: e.tensor_tensor(v4(T.t4), v4(T.wre), st4, ALU.mult), reads=[R("wre"), "s5c"], writes=[R("t4")])
        yield
        S.op("pool", lambda e: e.tensor_tensor(xr_b, T.bur, T.t2, ALU.subtract), reads=[R("bur"), R("t2")], writes=[("xr_b", k)])
        S.op("pool", lambda e: e.tensor_tensor(xi_b, T.bui, T.t4, ALU.add), reads=[R("bui"), R("t4")], writes=[("xi_b", k)])
        yield

    def s5_back(t, gp, j):
        c = gp // 4
        yb = 4 + (c % 2)
        s5u = s5u2[0]
        sres = ("s5u", 0, c)
        xr_b, xi_b = xr_b2[gp % 2], xi_b2[gp % 2]
        xrr, xir = ("xr_b", gp % 2), ("xi_b", gp % 2)
        q4 = gp % 4
        S.op("pe", lambda e: e.matmul(ps[yb][:, :], Clhs[:, gp, 0, :], xr_b, start=(q4 == 0), stop=False),
             reads=["s5c", xrr], writes=[("ps", yb)], inc=False)
        S.op("pe", lambda e: e.matmul(ps[yb][:, :], Clhs[:, gp, 1, :], xi_b, start=False, stop=(q4 == 3 or gp == NGP - 1)),
             reads=["s5c", xir], writes=[("ps", yb)])
        if q4 == 3 or gp == NGP - 1:
            S.op("dve", lambda e: e.scalar_tensor_tensor(yt, s5u[:, c, :], d_l[:, c:c + 1], ps[yb][:, :], ALU.mult, ALU.add),
                 reads=[sres, "small", ("ps", yb)], writes=["yt"])
            if t % 2 == 0:
                S.op("dve", lambda e: e.tensor_scalar(ysel[:, c, :], yt, sel[:, j:j + 1], None, ALU.mult),
                     reads=["yt", "small"], writes=[("ysel", c)])
            else:
                S.op("dve", lambda e: e.scalar_tensor_tensor(ysel[:, c, :], yt, nsel[:, j:j + 1], ysel[:, c, :], ALU.mult, ALU.add),
                     reads=["yt", "small", ("ysel", c)], writes=[("ysel", c)])

    def s5_gen(t):
        j = t // 2
        pending = []
        for g0 in range(0, NGP, 2):
            gens = [s5_front_gen(t, g0, j), s5_front_gen(t, g0 + 1, j)]
            alive = list(gens)
            n_mid = 0
            while alive:
                for g in list(alive):
                    try:
                        r = next(g)
                    except StopIteration:
                        alive.remove(g)
                        continue
                    if r == "mid":
                        n_mid += 1
                        if n_mid == 2:
                            for gp_ in pending:
                                s5_back(t, gp_, j)
                            pending = []
            pending = [g0, g0 + 1]
            yield
        for gp_ in pending:
            s5_back(t, gp_, j)
        if t % 2 == 1:
            S.dma("sp", yS_d[j], ysel, reads=[("ysel", c) for c in range(SC)], writes=[("yS", j)])

    def make_tick(gen):
        def _t():
            try:
                next(gen)
            except StopIteration:
                tick[0] = None
        return _t

    def drain(gen):
        for _ in gen:
            pass

    pend = [None]
    for t in range(NT):
        j = t // 2
        norm1(xs, t * TT, xst, xn)
        chk('seq_norm')
        ko_i = [0]

        def epi_k(ct, bank):
            kb_ = ko_i[0] % 2
            ko_i[0] += 1
            qknorm(bank, kn_c, "small", kout[kb_], ("kout", kb_), (sqb, rstd))
            S.dma("sp", kT_d[ct, :, t * TT:(t + 1) * TT], kout[kb_], reads=[("kout", kb_)], writes=[("kT", ct, t)])
        tick_on[0] = 'k' not in DBG
        gemm_fm(uT, "uT", DC, w_in, cfg.COL_K, FW, epi_k, gcols=256, bank_sets=((0, 1), (2, 3), (4, 5)))
        chk('seq_k')
        vs_i = [0]

        def epi_v(cb, tb, bank, ncb):
            kb = t * 4 + tb
            vb_ = vs_i[0] % 2
            vs_i[0] += 1
            nh_ = ncb // 128
            S.op("act", lambda e: e.activation(vstg[vb_][:, 0:ncb], ps[bank][:, 0:ncb], AF.Copy),
                 reads=[("ps", bank)], writes=[("vstg", vb_)])
            S.dma("sp", vS_d[cb * 4:cb * 4 + nh_, :, kb, :].rearrange("h p d -> p h d"),
                  vstg[vb_][:, 0:ncb].rearrange("p (h d) -> p h d", d=128),
                  reads=[("vstg", vb_)], writes=[("vS", cb, kb)])
        tick_on[0] = 'v' not in DBG
        gemm_tm(uT, "uT", DC, w_in, cfg.COL_V, FW, epi_v)

        chk('seq_v')
        def epi_f(cb, tb, bank, ncb):
            kb = t * 4 + tb
            fl = ftmp[:, 0, :]
            fe = ftmp[:, 1, :]
            lf = ftmp[:, 2, :]
            if t % 2 == 0 and tb == 0:
                S.op("dve", lambda e: e.tensor_copy(Fref[:, j, :], carry), reads=["carry"], writes=["Fref"])
            S.op("dve", lambda e: e.tensor_tensor(fl, ps[bank][:, 0:NH], bf_rep, ALU.add),
                 reads=[("ps", bank), "small"], writes=["fl"])
            S.op("act", lambda e: e.activation(fe, fl, AF.Exp, scale=-1.0), reads=["fl"], writes=["fe"])
            S.op("act", lambda e: e.activation(lf, fe, AF.Ln, bias=1.0), reads=["fe"], writes=["lf"])
            S.op("pe", lambda e: e.matmul(ps[bank][:, 128:128 + NH], tri_f, lf, start=True, stop=True),
                 reads=["consts", "lf"], writes=[("ps", bank)], inc=False)
            S.op("pe", lambda e: e.matmul(ps[bank][:, 256:256 + NH], ones_f, lf, start=True, stop=True),
                 reads=["ones_f", "lf"], writes=[("ps", bank)])
            S.op("dve", lambda e: e.tensor_tensor(Fall[:, kb, :], carry, ps[bank][:, 128:128 + NH], ALU.subtract),
                 reads=["carry", ("ps", bank)], writes=["Fall"])
            S.op("dve", lambda e: e.tensor_tensor(carry, carry, ps[bank][:, 256:256 + NH], ALU.subtract),
                 reads=["carry", ("ps", bank)], writes=["carry"])
        tick_on[0] = 'f' not in DBG
        gemm_tm(uT, "uT", DC, w_in, cfg.COL_F, NH, epi_f)
        chk('seq_f')

        def epi_s5(ct, bank):
            S.op("act", lambda e: e.activation(s5u2[0][:, ct, :], ps[bank][:, :], AF.Copy), reads=[("ps", bank)],
                 writes=[("s5u", 0, ct)])
        tick_on[0] = False
        gemm_fm(uT, "uT", DC, w_in, cfg.COL_S5, SW, epi_s5, bank_sets=((4, 5, 6, 7),))
        tick_on[0] = True
        chk('seq_s5in')
        if pend[0] is not None:
            drain(pend[0])
        pend[0] = s5_gen(t)
        tick[0] = make_tick(pend[0])
        tick[0] = None
        drain(pend[0])
        pend[0] = None
        chk('seq_s5')
    tick[0] = None
    S.barrier()
    seq_peak = A.peak
    chk('seq')

    TRB[:] = [6, 7]
    NRB[:] = [6, 7]
    TR_ALL_ACT[0] = False
    A.reset(m_base)
    attnT = A.alloc([128, NH, TT], BF16)
    ssmT = A.alloc([128, SC, TT], BF16)
    mergedT = A.alloc([128, DC, TT], BF16)
    m_own = A.mark()

    for j in range(NJ):
        nkb = (2 * j + 2) * 4
        A.reset(m_own)
        QT = A.alloc([128, NH, TT], BF16)
        m_a = A.mark()
        xst = A.alloc([128, D], F32)
        xn = A.alloc([128, D], BF16)
        sqb = [A.alloc([128, TT], BF16) for _ in range(2)]
        rstd = [A.alloc([128, TT], F32) for _ in range(2)]
        norm1(xo, j * TT, xst, xn)

        def epi_q(ct, bank):
            qknorm(bank, qn_s, "qn_s", QT[:, ct, :], ("QT", ct), (sqb, rstd))
        gemm_fm(uT, "uT", DC, w_in, 0, FW, epi_q, gcols=256, bank_sets=((0, 1), (2, 3), (4, 5)))
        S.barrier()
        chk('own_q')
        A.reset(m_a)
        kst = [A.alloc([128, SEQ], BF16) for _ in range(2)]
        vst = [A.alloc([128, NKB, 128], BF16) for _ in range(2)]
        maskadd = A.alloc([128, 8, TT], BF16)
        qposb = A.alloc([128, TT], F32)
        mk = A.alloc([128, TT], F32)
        biasK = A.alloc([128, NKB, NH], F32)
        fqt = A.alloc([128, 4, 48], F32)
        fqh = A.alloc([128, TT], BF16)
        fqhf = A.alloc([128, TT], F32)
        fql = A.alloc([128, TT], BF16)
        fqhl = A.alloc([128, TT], BF16)
        PT = [A.alloc([128, TT], BF16) for _ in range(3)]
        mt = [A.alloc([128, TT], F32) for _ in range(2)]
        rc = A.alloc([128, TT], F32)
        S.dma("sp", qposb, qpos_d[:, j * TT:(j + 1) * TT], writes=["qposb"])
        S.op("dve", lambda e: e.memset(fqt, 0.0), writes=["fqt"])
        S.op("dve", lambda e: e.memset(fqhl, 0.0), writes=["fqhl"])
        for tb in range(4):
            fa = Fall[:, (2 * j) * 4 + tb, :]
            fb = Fall[:, (2 * j + 1) * 4 + tb, :]
            S.op("dve", lambda e: e.tensor_scalar(fqt[:, tb, 0:NH], fa, sel[:, j:j + 1], None, ALU.mult),
                 reads=["Fall", "small"], writes=["fqt"])
            S.op("dve", lambda e: e.scalar_tensor_tensor(fqt[:, tb, 0:NH], fb, nsel[:, j:j + 1], fqt[:, tb, 0:NH],
                                                         ALU.mult, ALU.add), reads=["Fall", "small", "fqt"], writes=["fqt"])
            S.op("dve", lambda e: e.tensor_tensor(fqt[:, tb, 0:NH], fqt[:, tb, 0:NH], Fref[:, j, :], ALU.subtract),
                 reads=["fqt", "Fref"], writes=["fqt"])
            S.op("dve", lambda e: e.tensor_copy(fqt[:, tb, 32:32 + NH], fqt[:, tb, 0:NH]), reads=["fqt"], writes=["fqt"])
        for tb in range(4):
            S.op("pe", lambda e: e.transpose(ps[7][0:48, tb * 128:(tb + 1) * 128], fqt[:, tb, :], ident_f),
                 reads=["fqt", "consts"], writes=[("ps", 7)], inc=(tb == 3))
        S.op("dve", lambda e: e.tensor_copy(fqh[0:48, :], ps[7][0:48, :]), reads=[("ps", 7)], writes=["fqh"])
        S.op("dve", lambda e: e.tensor_copy(fqhf[0:48, :], fqh[0:48, :]), reads=["fqh"], writes=["fqhf"])
        S.op("dve", lambda e: e.tensor_tensor(fql[0:48, :], ps[7][0:48, :], fqhf[0:48, :], ALU.subtract),
             reads=[("ps", 7), "fqhf"], writes=["fql"])
        S.op("dve", lambda e: e.tensor_copy(fqhl[0:16, :], fqh[0:16, :]), reads=["fqh", "fqhl"], writes=["fqhl"])
        S.op("dve", lambda e: e.tensor_copy(fqhl[32:48, :], fql[32:48, :]), reads=["fql", "fqhl"], writes=["fqhl"])
        S.op("dve", lambda e: e.tensor_tensor(biasK[:, 0:nkb, :], Fall[:, 0:nkb, :],
                                              Fref[:, j, :].unsqueeze(1).broadcast_to([128, nkb, NH]), ALU.subtract),
             reads=["Fall", "Fref"], writes=["biasK"])
        S.op("dve", lambda e: e.tensor_scalar(biasK[:, 0:nkb, :], biasK[:, 0:nkb, :], -1.0, None, ALU.mult),
             reads=["biasK"], writes=["biasK"])
        for mi in range(8):
            kb = nkb - 8 + mi
            S.op("dve", lambda e: e.tensor_scalar(mk, qposb, kpos[:, kb:kb + 1], None, ALU.is_ge),
                 reads=["qposb", "small"], writes=["mk"])
            S.op("dve", lambda e: e.tensor_scalar(maskadd[:, mi, :], mk, -1.0, 30000.0, ALU.add, ALU.mult),
                 reads=["mk"], writes=["maskadd"])
        for h in range(NH):
            hb = h % 2
            S.dma("sp", kst[hb][:, 0:nkb * 128], kT_d[h, :, 0:nkb * 128],
                  reads=[("kT", h, tt_) for tt_ in range(2 * j + 2)], writes=[("kst", hb)])
            S.dma("sp", vst[hb][:, 0:nkb, :], vS_d[h, :, 0:nkb, :],
                  reads=[("vS", h // 4, kk) for kk in range(nkb)], writes=[("vst", hb)])
            bO, bD = 3 + hb, 5 + hb

            def s_step(kb):
                sb_ = kb % 3
                S.op("pe", lambda e: e.matmul(ps[sb_][:, :], kst[hb][:, kb * 128:(kb + 1) * 128], QT[:, h, :],
                                              start=True, stop=False),
                     reads=[("kst", hb), ("QT", h)], writes=[("ps", sb_)], inc=False)
                S.op("pe", lambda e: e.matmul(ps[sb_][:, :], oneh[0:48, h, :], fqhl[0:48, :], start=False, stop=True),
                     reads=["oneh", "fqhl"], writes=[("ps", sb_)])
                pt = PT[sb_]
                if kb >= nkb - 8:
                    mi = kb - (nkb - 8)
                    m_ = mt[kb % 2]
                    S.op("dve", lambda e: e.scalar_tensor_tensor(m_, ps[sb_][:, :], biasK[:, kb, h:h + 1], maskadd[:, mi, :],
                                                                 ALU.add, ALU.add),
                         reads=[("ps", sb_), "biasK", "maskadd"], writes=[("mt", kb % 2)])
                    S.op("act", lambda e: e.activation(pt, m_, AF.Exp), reads=[("mt", kb % 2)], writes=[("PT", sb_)])
                else:
                    S.op("act", lambda e: e.activation(pt, ps[sb_][:, :], AF.Exp, bias=biasK[:, kb, h:h + 1]),
                         reads=[("ps", sb_), "biasK"], writes=[("PT", sb_)])

            def pv_step(kb):
                sb_ = kb % 3
                S.op("pe", lambda e: e.matmul(ps[bO][:, :], vst[hb][:, kb, :], PT[sb_], start=(kb == 0), stop=(kb == nkb - 1)),
                     reads=[("vst", hb), ("PT", sb_)], writes=[("ps", bO)], inc=False)
                S.op("pe", lambda e: e.matmul(ps[bD][:, :], ones_b, PT[sb_], start=(kb == 0), stop=(kb == nkb - 1)),
                     reads=["ones_b", ("PT", sb_)], writes=[("ps", bD)])
            for i in range(nkb + 2):
                if i < nkb:
                    s_step(i)
                if i >= 2:
                    pv_step(i - 2)
            S.op("dve", lambda e: e.reciprocal(rc, ps[bD][:, :]), reads=[("ps", bD)], writes=["rc"])
            S.op("dve", lambda e: e.tensor_tensor(attnT[:, h, :], ps[bO][:, :], rc, ALU.mult),
                 reads=[("ps", bO), "rc"], writes=[("attnT", h)])
        S.barrier()
        chk('attn')
        A.reset(m_a)
        ysT = A.alloc([128, SC, TT], BF16)
        gT_ = A.alloc([128, SC, TT], BF16)
        g1 = A.alloc([128, TT], F32)
        g2 = A.alloc([128, TT], F32)
        sig = A.alloc([128, TT], F32)
        gfT = A.alloc([128, 4, TT], BF16)
        gsT = A.alloc([128, 4, TT], BF16)
        mtmp = A.alloc([128, 4, TT], F32)
        mt2 = A.alloc([128, TT], F32)
        S.dma("sp", ysT, yS_d[j], reads=[("yS", j)], writes=["ysT"])
        for c in range(SC):
            yc = ysT[:, c, :]
            S.op("dve", lambda e: e.tensor_tensor(g1, yc, yc, ALU.mult), reads=["ysT"], writes=["g1"])
            S.op("dve", lambda e: e.tensor_scalar(g1, g1, 0.044715, 1.0, ALU.mult, ALU.add), reads=["g1"], writes=["g1"])
            S.op("dve", lambda e: e.tensor_tensor(g1, g1, yc, ALU.mult), reads=["g1", "ysT"], writes=["g1"])
            S.op("act", lambda e: e.activation(g2, g1, AF.Sigmoid, scale=2.0 * math.sqrt(2.0 / math.pi)), reads=["g1"],
                 writes=["g2"])
            S.op("dve", lambda e: e.tensor_tensor(gT_[:, c, :], g2, yc, ALU.mult),
                 reads=["g2", "ysT"], writes=[("gT", c)])

        def epi_glu(ct, bank):
            S.op("act", lambda e: e.activation(sig, ps[bank][:, :], AF.Sigmoid, bias=bglu_l[:, ct:ct + 1]),
                 reads=[("ps", bank), "small"], writes=["sig"])
            S.op("dve", lambda e: e.tensor_tensor(ssmT[:, ct, :], gT_[:, ct, :], sig, ALU.mult),
                 reads=["sig", ("gT", ct)], writes=[("ssmT", ct)])
        gemm_fm(gT_, "gT", SC, w_glu, 0, SW, epi_glu)
        gi = 0
        for cg in range(0, D, 512):
            c0t = cg // 128

            def epi_gf(ct, bank):
                S.op("act", lambda e: e.activation(gfT[:, ct, :], ps[bank][:, :], AF.Sigmoid, bias=bgT[:, c0t + ct:c0t + ct + 1]),
                     reads=[("ps", bank), "small"], writes=[("gfT", ct)])

            def epi_gs(ct, bank):
                S.op("act", lambda e: e.activation(gsT[:, ct, :], ps[bank][:, :], AF.Sigmoid,
                                                   bias=bgT[:, DC + c0t + ct:DC + c0t + ct + 1]),
                     reads=[("ps", bank), "small"], writes=[("gsT", ct)])

            def epi_pf(ct, bank):
                S.op("dve", lambda e: e.tensor_tensor(mtmp[:, ct, :], ps[bank][:, :], gfT[:, ct, :], ALU.mult),
                     reads=[("ps", bank), ("gfT", ct)], writes=[("mtmp", ct)])

            def epi_ps(ct, bank):
                S.op("dve", lambda e: e.tensor_tensor(mt2, ps[bank][:, :], gsT[:, ct, :], ALU.mult),
                     reads=[("ps", bank), ("gsT", ct)], writes=["mt2"])
                S.op("pool", lambda e: e.tensor_tensor(mergedT[:, c0t + ct, :], mt2, mtmp[:, ct, :], ALU.add),
                     reads=["mt2", ("mtmp", ct)], writes=[("mergedT", c0t + ct)])
            gi = gemm_fm(uT, "uT", DC, w_in, cfg.COL_GF + cg, 512, epi_gf, gi0=gi)
            gi = gemm_fm(uT, "uT", DC, w_in, cfg.COL_GS + cg, 512, epi_gs, gi0=gi)
            gi = gemm_fm(attnT, "attnT", NH, w_pf, cg, 512, epi_pf, gi0=gi)
            gi = gemm_fm(ssmT, "ssmT", SC, w_ps, cg, 512, epi_ps, gi0=gi)
        S.barrier()
        chk('B')
        A.reset(m_a)
        xres = [A.alloc([128, 512], F32) for _ in range(2)]
        htmp = [A.alloc([128, 512], F32) for _ in range(2)]
        junk = A.alloc([128, 512], BF16)
        h16 = A.alloc([128, 4, D], BF16)
        ssq2 = A.alloc([128, 4, 8], F32)
        NCB = D // 512
        S.op("dve", lambda e: e.memset(ssq2, 0.0), writes=["ssq2"])
        xi_ = [0]

        def epi_wo(cb, tb, bank, ncb):
            b_ = xi_[0] % 2
            xi_[0] += 1
            r0 = j * TT + tb * 128
            S.dma("sp", xres[b_], xo[r0:r0 + 128, cb * 512:(cb + 1) * 512], writes=[("xres", b_)])
            S.op("dve", lambda e: e.tensor_tensor(htmp[b_], ps[bank][:, :], xres[b_], ALU.add),
                 reads=[("ps", bank), ("xres", b_)], writes=[("htmp", b_)])
            S.dma("sp", hS_d[r0:r0 + 128, cb * 512:(cb + 1) * 512], htmp[b_], reads=[("htmp", b_)],
                  writes=[("hS", tb, cb)])
            S.op("act", lambda e: e.activation(h16[:, tb, cb * 512:(cb + 1) * 512], htmp[b_], AF.Copy),
                 reads=[("htmp", b_)], writes=[("h16", tb)])
            S.op("act", lambda e: e.activation(junk, htmp[b_], AF.Square, accum_out=ssq2[:, tb, cb:cb + 1]),
                 reads=[("htmp", b_), "ssq2"], writes=["junk", "ssq2"])
        gemm_tm(mergedT, "mergedT", DC, w_out, 0, D, epi_wo)
        for tb in range(4):
            ssq = stat[:, 0:1]
            S.op("dve", lambda e: e.reduce_sum(ssq, ssq2[:, tb, 0:NCB], AX.X), reads=["ssq2", "rs"], writes=["rs"])
            rstd_from(ssq, 1.0 / D)
            S.op("dve", lambda e: e.tensor_scalar(h16[:, tb, :], h16[:, tb, :], ssq, None, ALU.mult),
                 reads=[("h16", tb), "rs"], writes=[("h16", tb)])
            transpose_to_uT(h16[:, tb, :], ("h16", tb), tb, gffnT, "small")
        S.barrier()
        chk('C')
        A.reset(m_base)
        actT = A.alloc([128, KF, TT], BF16)
        sgT = A.alloc([128, 4, TT], BF16)
        hres = [A.alloc([128, 512], F32) for _ in range(2)]
        ost = [A.alloc([128, 512], F32) for _ in range(2)]
        gi = 0
        for cg in range(0, DFF, 512):
            ncg = min(512, DFF - cg)
            c0t = cg // 128

            def epi_g(ct, bank):
                S.op("act", lambda e: e.activation(sgT[:, ct, :], ps[bank][:, :], AF.Silu), reads=[("ps", bank)],
                     writes=[("sgT", ct)])

            def epi_u(ct, bank):
                S.op("dve", lambda e: e.tensor_tensor(actT[:, c0t + ct, :], ps[bank][:, :], sgT[:, ct, :], ALU.mult),
                     reads=[("ps", bank), ("sgT", ct)], writes=[("actT", c0t + ct)])
            gi = gemm_fm(uT, "uT", DC, w_gu, cg, ncg, epi_g, gi0=gi)
            gi = gemm_fm(uT, "uT", DC, w_gu, DFF + cg, ncg, epi_u, gi0=gi)
        oi_ = [0]

        def epi_dn(cb, tb, bank, ncb):
            b_ = oi_[0] % 2
            oi_[0] += 1
            r0 = j * TT + tb * 128
            S.dma("sp", hres[b_], hS_d[r0:r0 + 128, cb * 512:(cb + 1) * 512], reads=[("hS", tb, cb)],
                  writes=[("hres", b_)])
            S.op("dve", lambda e: e.tensor_tensor(ost[b_], ps[bank][:, :], hres[b_], ALU.add),
                 reads=[("ps", bank), ("hres", b_)], writes=[("ost", b_)])
            S.dma("sp", y_d[r0:r0 + 128, cb * 512:(cb + 1) * 512], ost[b_], reads=[("ost", b_)], writes=[("y", r0, cb)])
        gemm_tm(actT, "actT", KF, w_dn, 0, D, epi_dn)
        S.barrier()
    S.barrier()
    info = dict(ops=S.n_ops, waits=S.n_wait, seq_peak=seq_peak, peak=A.peak)
    return nc, info


OWN_TILES = ((0, 3, 4, 7), (1, 2, 5, 6))


def make_in_maps(cfg, inp, n_pairs):
    D, NH, G, DFF = cfg.D, cfg.NH, cfg.G, cfg.DFF
    DC, SC, NGP = cfg.DC, cfg.SC, cfg.NGP
    f32 = np.float32

    def colT(v, n):
        return np.ascontiguousarray(np.asarray(v, f32).reshape(n, 128).T)

    def rep(v):
        v = np.asarray(v, f32)
        return np.ascontiguousarray(np.broadcast_to(v[None, :], (128, v.shape[0])))
    consts = np.zeros((128, 512), f32)
    consts[:, 0:128] = np.eye(128, dtype=f32)
    consts[:, 128:256] = np.triu(np.ones((128, 128), f32))
    consts[:, 256:384] = np.arange(1, 129, dtype=f32)[None, :]
    oneh = np.zeros((128, NH, 128), f32)
    for h in range(NH):
        oneh[h, h, :] = 1.0
        oneh[32 + h, h, :] = 1.0
    oneh = oneh.reshape(128, NH * 128).astype(ml_dtypes.bfloat16)
    kpos = (np.arange(NKB, dtype=f32)[None, :] * 128 + np.arange(128, dtype=f32)[:, None])

    def pairl(a):
        return np.asarray(a, f32).reshape(NGP, 2, 64).transpose(1, 2, 0).reshape(128, NGP)
    lam_re, lam_im = inp["s5_lambda_re"][0], inp["s5_lambda_im"][0]
    ls = np.broadcast_to(np.asarray(inp["s5_log_step"][0], f32)[:, None], (G, 64))
    bre = np.asarray(inp["s5_b_re"][0], f32).reshape(NGP, 2, 64, 16).transpose(1, 2, 0, 3).reshape(128, NGP * 16)
    bim = np.asarray(inp["s5_b_im"][0], f32).reshape(NGP, 2, 64, 16).transpose(1, 2, 0, 3).reshape(128, NGP * 16)
    cre = np.asarray(inp["s5_c_re"][0], f32).reshape(NGP, 2, 16, 64).transpose(1, 3, 0, 2).reshape(128, NGP * 16)
    cim = np.asarray(inp["s5_c_im"][0], f32).reshape(NGP, 2, 16, 64).transpose(1, 3, 0, 2).reshape(128, NGP * 16)
    s5p = np.ascontiguousarray(np.concatenate([pairl(lam_re), pairl(lam_im), pairl(ls), bre, bim, cre, cim], axis=1))
    shared = {
        "w_in": np.ascontiguousarray(inp["w_in"][0], dtype=f32),
        "w_glu": np.ascontiguousarray(inp["w_glu"][0], dtype=f32),
        "w_pf": np.ascontiguousarray(inp["w_proj_fox"][0], dtype=f32),
        "w_ps": np.ascontiguousarray(inp["w_proj_s5"][0], dtype=f32),
        "w_out": np.ascontiguousarray(inp["w_out"][0], dtype=f32),
        "w_gu": np.ascontiguousarray(inp["w_gate_up"][0], dtype=f32),
        "w_dn": np.ascontiguousarray(inp["w_down"][0], dtype=f32),
        "consts": consts, "oneh": oneh, "s5p": s5p,
    }
    in_maps = []
    x = np.asarray(inp["x"], f32)
    for c in range(2 * n_pairs):
        b, half = c // 2, c % 2
        own = OWN_TILES[half]
        selv = np.array([1.0 if own[j] == 2 * j else 0.0 for j in range(NJ)], f32)
        small = np.concatenate([
            colT(inp["g_mix"][0], DC), colT(inp["g_ffn"][0], DC), np.zeros((128, DC), f32),
            colT(inp["b_gates"][0], 2 * DC), colT(inp["q_norm"][0], 1), colT(inp["k_norm"][0], 1),
            rep(inp["b_fgate"][0]), rep(selv), rep(1.0 - selv), kpos,
            colT(np.asarray(inp["s5_d"][0], f32).reshape(-1), SC), colT(inp["b_glu"][0], SC)], axis=1)
        qpos = np.concatenate([np.arange(t * TT, (t + 1) * TT, dtype=f32) for t in own])
        m = dict(shared)
        m["xs"] = np.ascontiguousarray(x[b])
        m["xo"] = np.ascontiguousarray(np.concatenate([x[b, t * TT:(t + 1) * TT] for t in own], axis=0))
        m["small"] = np.ascontiguousarray(small, dtype=f32)
        m["qpos"] = np.ascontiguousarray(np.broadcast_to(qpos[None, :], (128, NJ * TT)), dtype=f32)
        in_maps.append(m)
    return in_maps


def gather(cfg, results, n_pairs):
    out = np.zeros((n_pairs, SEQ, cfg.D), np.float32)
    for c in range(2 * n_pairs):
        b, half = c // 2, c % 2
        y = results[c]["y"]
        for j, t in enumerate(OWN_TILES[half]):
            out[b, t * TT:(t + 1) * TT] = y[j * TT:(j + 1) * TT]
    return out


def kernel(**inputs):
    cfg = Cfg()
    nc, info = build(cfg)
    in_maps = make_in_maps(cfg, inputs, 4)
    res = run_bass_kernel_spmd(nc, in_maps, core_ids=list(range(8)))
    return gather(cfg, res.results, 4)
```
